# Optimizing a Trainium2 kernel written in Bass

```python
import math
import jax
import jax.numpy as jnp
from jax import lax
import numpy as np

D_MODEL = 2048
BATCH = 4
SEQ = 4096
DEPTH = 4

HEAD_DIM = 128
FOX_HEADS = 4
DIFF_HEADS = 4
DIFF_HALF = HEAD_DIM // 2
NSA_HEADS = 8
NSA_KV_HEADS = 2
CMP_BLOCK = 32
CMP_STRIDE = 16
CMP_HIDDEN = 256
SEL_BLOCK = 64
SEL_TOPK = 16
WINDOW = 512
Q_BLOCK = 128
D_FF = ((8 * D_MODEL + 3 * 256 - 1) // (3 * 256)) * 256
FOX_WIDTH = FOX_HEADS * HEAD_DIM
DIFF_WIDTH = DIFF_HEADS * HEAD_DIM
NSA_WIDTH = NSA_HEADS * HEAD_DIM
NSA_KV_WIDTH = NSA_KV_HEADS * HEAD_DIM
MIX_WIDTH = FOX_WIDTH + DIFF_WIDTH + NSA_WIDTH
N_BRANCHES = 3
IN_SPLITS = (FOX_WIDTH, FOX_WIDTH, FOX_WIDTH, FOX_HEADS,
             DIFF_WIDTH, DIFF_WIDTH, DIFF_WIDTH,
             NSA_WIDTH, NSA_KV_WIDTH, NSA_KV_WIDTH, NSA_KV_WIDTH, NSA_KV_WIDTH,
             NSA_KV_WIDTH, NSA_KV_WIDTH, 3 * NSA_HEADS)
IN_COLS = sum(IN_SPLITS)
EPS = 1e-6
NEG_INF = -1e30
FORCE_SCORE = 1e4
FORGET_BIAS_CENTER = 3.0

kernel_name = "fox_diff_nsa_gated_hybrid"


def _rmsnorm(x, g):
    xf = x.astype(jnp.float32)
    y = xf * lax.rsqrt(jnp.mean(xf * xf, axis=-1, keepdims=True) + EPS)
    return (y * g.astype(jnp.float32)).astype(x.dtype)


def _alibi_slopes(n):
    return 2.0 ** (-8.0 * jnp.arange(1, n + 1, dtype=jnp.float32) / n)


def _split_cols(t, widths):
    out, start = [], 0
    for w in widths:
        out.append(t[..., start:start + w])
        start += w
    return out


def _heads(t, n):
    b, s, _ = t.shape
    return t.reshape(b, s, n, -1).transpose(0, 2, 1, 3)


def _merge_heads(t):
    b, n, s, d = t.shape
    return t.transpose(0, 2, 1, 3).reshape(b, s, n * d)


def _fox_attention(q, k, v, log_f):
    b, h, s, d = q.shape
    nb = s // Q_BLOCK
    scale = d ** -0.5
    cum = jnp.cumsum(log_f, axis=-1)
    kpos = jnp.arange(s)
    q_blk = q.reshape(b, h, nb, Q_BLOCK, d).transpose(2, 0, 1, 3, 4)
    c_blk = cum.reshape(b, h, nb, Q_BLOCK).transpose(2, 0, 1, 3)

    def block(args):
        i, qi, ci = args
        qpos = i * Q_BLOCK + jnp.arange(Q_BLOCK)
        sc = jnp.einsum('bhqd,bhkd->bhqk', qi, k).astype(jnp.float32) * scale
        sc = sc + ci[..., :, None] - cum[..., None, :]
        sc = jnp.where(kpos[None, :] <= qpos[:, None], sc, NEG_INF)
        p = jax.nn.softmax(sc, axis=-1).astype(v.dtype)
        return jnp.einsum('bhqk,bhkd->bhqd', p, v)

    o = lax.map(block, (jnp.arange(nb), q_blk, c_blk))
    return o.transpose(1, 2, 0, 3, 4).reshape(b, h, s, d)


def _diff_attention(q, k, v, lam, lam_init, slopes, subln):
    b, h, s, _, dh = q.shape
    d = v.shape[-1]
    nb = s // Q_BLOCK
    scale = dh ** -0.5
    kpos = jnp.arange(s)
    sl = slopes[None, :, None, None, None]
    q_blk = q.reshape(b, h, nb, Q_BLOCK, 2, dh).transpose(2, 0, 1, 3, 4, 5)

    def block(args):
        i, qi = args
        qpos = i * Q_BLOCK + jnp.arange(Q_BLOCK)
        dist = qpos[:, None] - kpos[None, :]
        sc = jnp.einsum('bhqcd,bhkcd->bhcqk', qi, k).astype(jnp.float32) * scale
        sc = jnp.where(dist >= 0, sc - sl * dist.astype(jnp.float32), NEG_INF)
        p = jax.nn.softmax(sc, axis=-1)
        a = p[:, :, 0] - lam * p[:, :, 1]
        return jnp.einsum('bhqk,bhkd->bhqd', a.astype(v.dtype), v)

    o = lax.map(block, (jnp.arange(nb), q_blk))
    o = o.transpose(1, 2, 0, 3, 4).reshape(b, h, s, d)
    return _rmsnorm(o, subln) * (1.0 - lam_init)


def _compress(kv, pos, w1, w2):
    s = kv.shape[2]
    nc = (s - CMP_BLOCK) // CMP_STRIDE + 1
    idx = jnp.arange(nc)[:, None] * CMP_STRIDE + jnp.arange(CMP_BLOCK)[None, :]
    blocks = kv[:, :, idx, :] + pos
    flat = blocks.reshape(blocks.shape[0], blocks.shape[1], nc, CMP_BLOCK * HEAD_DIM)
    return jax.nn.silu(flat @ w1) @ w2


def _nsa_attention(q, k_c, v_c, k_s, v_s, k_w, v_w, gates, slopes):
    b, h, s, d = q.shape
    g = k_s.shape[1]
    r = h // g
    nb = s // Q_BLOCK
    nc = k_c.shape[2]
    nsel = s // SEL_BLOCK
    topk = min(SEL_TOPK, nsel)
    scale = d ** -0.5
    cmp_start = jnp.arange(nc) * CMP_STRIDE
    cmp_end = cmp_start + CMP_BLOCK - 1
    sel_start = jnp.arange(nsel) * SEL_BLOCK
    overlap = ((cmp_start[:, None] < sel_start[None, :] + SEL_BLOCK)
               & (cmp_end[:, None] >= sel_start[None, :])).astype(jnp.float32)
    blk_id = jnp.arange(nsel)
    sl = slopes.reshape(g, r)[None, :, :, None, None]
    q_blk = q.reshape(b, g, r, nb, Q_BLOCK, d).transpose(3, 0, 1, 2, 4, 5)
    g_blk = gates.reshape(b, g, r, nb, Q_BLOCK, 3).transpose(3, 0, 1, 2, 4, 5)
    pad = jnp.zeros((b, g, WINDOW, d), k_w.dtype)
    kw_pad = jnp.concatenate([pad, k_w], axis=2)
    vw_pad = jnp.concatenate([pad, v_w], axis=2)
    bi = jnp.arange(b)[:, None, None, None]
    gidx = jnp.arange(g)[None, :, None, None]

    def block(args):
        i, qi, gate_i = args
        qpos = i * Q_BLOCK + jnp.arange(Q_BLOCK)
        dist_c = qpos[:, None] - cmp_end[None, :]
        valid_c = dist_c >= 0
        sc = jnp.einsum('bgrqd,bgcd->bgrqc', qi, k_c).astype(jnp.float32) * scale
        sc = jnp.where(valid_c, sc - sl * dist_c.astype(jnp.float32), NEG_INF)
        p_c = jax.nn.softmax(sc, axis=-1) * jnp.any(valid_c, axis=-1)[:, None]
        o_cmp = jnp.einsum('bgrqc,bgcd->bgrqd', p_c.astype(v_c.dtype), v_c)
        imp = jnp.einsum('bgrqc,cj->bgqj', p_c, overlap)
        cur = qpos // SEL_BLOCK
        valid_s = blk_id[None, :] <= cur[:, None]
        forced = ((blk_id[None, :] == 0) | (blk_id[None, :] == cur[:, None])
                  | (blk_id[None, :] == cur[:, None] - 1))
        imp = jnp.where(valid_s, jnp.where(forced, FORCE_SCORE, imp), -1.0)
        _, top = lax.top_k(imp, topk)
        tok = (top[..., None] * SEL_BLOCK + jnp.arange(SEL_BLOCK)).reshape(b, g, Q_BLOCK, topk * SEL_BLOCK)
        ks = k_s[bi, gidx, tok]
        vs = v_s[bi, gidx, tok]
        dist_s = (qpos[None, None, :, None] - tok)[:, :, None]
        ss = jnp.einsum('bgrqd,bgqnd->bgrqn', qi, ks).astype(jnp.float32) * scale
        ss = jnp.where(dist_s >= 0, ss - sl * dist_s.astype(jnp.float32), NEG_INF)
        o_sel = jnp.einsum('bgrqn,bgqnd->bgrqd', jax.nn.softmax(ss, axis=-1).astype(vs.dtype), vs)
        kw = lax.dynamic_slice_in_dim(kw_pad, i * Q_BLOCK, WINDOW + Q_BLOCK, axis=2)
        vw = lax.dynamic_slice_in_dim(vw_pad, i * Q_BLOCK, WINDOW + Q_BLOCK, axis=2)
        kpos = i * Q_BLOCK - WINDOW + jnp.arange(WINDOW + Q_BLOCK)
        dist_w = qpos[:, None] - kpos[None, :]
        valid_w = (dist_w >= 0) & (dist_w < WINDOW) & (kpos[None, :] >= 0)
        sw = jnp.einsum('bgrqd,bgkd->bgrqk', qi, kw).astype(jnp.float32) * scale
        sw = jnp.where(valid_w, sw - sl * dist_w.astype(jnp.float32), NEG_INF)
        o_win = jnp.einsum('bgrqk,bgkd->bgrqd', jax.nn.softmax(sw, axis=-1).astype(vw.dtype), vw)
        return gate_i[..., 0:1] * o_cmp + gate_i[..., 1:2] * o_sel + gate_i[..., 2:3] * o_win

    o = lax.map(block, (jnp.arange(nb), q_blk, g_blk))
    return o.transpose(1, 2, 3, 0, 4, 5).reshape(b, h, s, d)


def _hybrid_mixer(h, w_in, f_bias, lam_vec, subln, cmp_pos, cmp_w1, cmp_w2,
                  wb_fox, wb_diff, wb_nsa, w_gate, w_out, lam_init):
    b, s, _ = h.shape
    (fq, fk, fv, ff, dq, dk, dv, nq, nkc, nvc, nks, nvs, nkw, nvw, ng) = _split_cols(h @ w_in, IN_SPLITS)
    log_f = jax.nn.log_sigmoid(ff.astype(jnp.float32) + f_bias.astype(jnp.float32)).transpose(0, 2, 1)
    o_fox = _fox_attention(_heads(fq, FOX_HEADS), _heads(fk, FOX_HEADS), _heads(fv, FOX_HEADS), log_f)
    lv = lam_vec.astype(jnp.float32)
    lam = jnp.exp(jnp.sum(lv[0] * lv[1])) - jnp.exp(jnp.sum(lv[2] * lv[3])) + lam_init
    dq2 = dq.reshape(b, s, DIFF_HEADS, 2, DIFF_HALF).transpose(0, 2, 1, 3, 4)
    dk2 = dk.reshape(b, s, DIFF_HEADS, 2, DIFF_HALF).transpose(0, 2, 1, 3, 4)
    o_diff = _diff_attention(dq2, dk2, _heads(dv, DIFF_HEADS), lam, lam_init,
                             _alibi_slopes(DIFF_HEADS), subln)
    k_cmp = _compress(_heads(nkc, NSA_KV_HEADS), cmp_pos[0], cmp_w1[0], cmp_w2[0])
    v_cmp = _compress(_heads(nvc, NSA_KV_HEADS), cmp_pos[1], cmp_w1[1], cmp_w2[1])
    nsa_gates = jax.nn.sigmoid(ng).reshape(b, s, NSA_HEADS, 3).transpose(0, 2, 1, 3)
    o_nsa = _nsa_attention(_heads(nq, NSA_HEADS), k_cmp, v_cmp,
                           _heads(nks, NSA_KV_HEADS), _heads(nvs, NSA_KV_HEADS),
                           _heads(nkw, NSA_KV_HEADS), _heads(nvw, NSA_KV_HEADS),
                           nsa_gates, _alibi_slopes(NSA_HEADS))
    y_fox = _merge_heads(o_fox) @ wb_fox
    y_diff = _merge_heads(o_diff) @ wb_diff
    y_nsa = _merge_heads(o_nsa) @ wb_nsa
    gate = jax.nn.sigmoid(h @ w_gate).reshape(b, s, N_BRANCHES, D_MODEL)
    merged = gate[:, :, 0] * y_fox + gate[:, :, 1] * y_diff + gate[:, :, 2] * y_nsa
    return merged @ w_out


def _swiglu(h, w_up, w_down):
    gate, up = _split_cols(h @ w_up, (D_FF, D_FF))
    return (jax.nn.silu(gate) * up) @ w_down


def setup_inputs(seed: int = 0) -> dict:
    key = jax.random.key(seed)
    ks = jax.random.split(key, 16)
    L, D = DEPTH, D_MODEL

    def nrm(k, shape, scale):
        return jax.random.normal(k, shape, jnp.float32) * scale

    return {
        "x": nrm(ks[0], (BATCH, SEQ, D), 1.0),
        "w_in": nrm(ks[1], (L, D, IN_COLS), D ** -0.5),
        "fox_forget_bias": FORGET_BIAS_CENTER + nrm(ks[2], (L, FOX_HEADS), 0.5),
        "diff_lambda": nrm(ks[3], (L, 4, DIFF_HALF), 0.1),
        "diff_subln": 1.0 + nrm(ks[4], (L, HEAD_DIM), 0.05),
        "nsa_cmp_pos": nrm(ks[5], (L, 2, CMP_BLOCK, HEAD_DIM), 0.1),
        "nsa_cmp_w1": nrm(ks[6], (L, 2, CMP_BLOCK * HEAD_DIM, CMP_HIDDEN), (CMP_BLOCK * HEAD_DIM) ** -0.5),
        "nsa_cmp_w2": nrm(ks[7], (L, 2, CMP_HIDDEN, HEAD_DIM), CMP_HIDDEN ** -0.5),
        "w_branch_fox": nrm(ks[8], (L, FOX_WIDTH, D), FOX_WIDTH ** -0.5),
        "w_branch_diff": nrm(ks[9], (L, DIFF_WIDTH, D), DIFF_WIDTH ** -0.5),
        "w_branch_nsa": nrm(ks[10], (L, NSA_WIDTH, D), NSA_WIDTH ** -0.5),
        "w_gate": nrm(ks[11], (L, D, N_BRANCHES * D), D ** -0.5),
        "w_out": nrm(ks[12], (L, D, D), D ** -0.5),
        "norm_gains": 1.0 + nrm(ks[13], (L, 4, D), 0.05),
        "w_ffn_up": nrm(ks[14], (L, D, 2 * D_FF), D ** -0.5),
        "w_ffn_down": nrm(ks[15], (L, D_FF, D), D_FF ** -0.5),
    }


def reference(x, w_in, fox_forget_bias, diff_lambda, diff_subln, nsa_cmp_pos, nsa_cmp_w1, nsa_cmp_w2,
              w_branch_fox, w_branch_diff, w_branch_nsa, w_gate, w_out, norm_gains, w_ffn_up, w_ffn_down):
    for l in range(DEPTH):
        lam_init = 0.8 - 0.6 * math.exp(-0.3 * l)
        h = _rmsnorm(x, norm_gains[l, 0])
        y = _hybrid_mixer(h, w_in[l], fox_forget_bias[l], diff_lambda[l], diff_subln[l],
                          nsa_cmp_pos[l], nsa_cmp_w1[l], nsa_cmp_w2[l],
                          w_branch_fox[l], w_branch_diff[l], w_branch_nsa[l],
                          w_gate[l], w_out[l], lam_init)
        x = x + _rmsnorm(y, norm_gains[l, 1])
        h = _rmsnorm(x, norm_gains[l, 2])
        x = x + _rmsnorm(_swiglu(h, w_ffn_up[l], w_ffn_down[l]), norm_gains[l, 3])
    return x
```

```python
import math
from contextlib import ExitStack
import numpy as np
import concourse.bass as bass
import concourse.mybir as mybir
from concourse.bass_utils import run_bass_kernel_spmd

F32 = mybir.dt.float32
BF16 = mybir.dt.bfloat16
I32 = mybir.dt.int32
AF = mybir.ActivationFunctionType
ALU = mybir.AluOpType

D = 2048
KC = 16
HD = 128
IN_COLS = 5660
DFF = 5632
EPS = 1e-6
NEG = -30000.0

C_FQ, C_FK, C_FV, C_FF = 0, 512, 1024, 1536
C_DQ, C_DK, C_DV = 1540, 2052, 2564
C_NQ, C_NKC, C_NVC, C_NKS, C_NVS, C_NKW, C_NVW, C_NG = 3076, 4100, 4356, 4612, 4868, 5124, 5380, 5636


class Sem:
    def __init__(self, h, name):
        self.h = h
        self.cnt = 0
        self.name = name


class Eng:
    def __init__(self, name, h, sem):
        self.name = name
        self.h = h
        self.sem = sem
        self.waited = {}


class Buf:
    def __init__(self, name, ap):
        self.name = name
        self.ap = ap
        self.w = {}
        self.r = {}
        self.dsem = None

    def __getitem__(self, idx):
        return V(self, self.ap[idx])

    def v(self, ap):
        return V(self, ap)

    @property
    def all(self):
        return V(self, self.ap)


class V:
    def __init__(self, buf, ap):
        self.buf = buf
        self.ap = ap

    def __getitem__(self, idx):
        return V(self.buf, self.ap[idx])


class KB:
    def __init__(self, nc):
        self.nc = nc
        self.es = ExitStack()
        self.eng = {}
        for n, h in [("pe", nc.tensor), ("act", nc.scalar), ("dve", nc.vector), ("pool", nc.gpsimd), ("sp", nc.sync)]:
            self.eng[n] = Eng(n, h, self.new_sem("e_" + n))
        self.dma_pool = [self.new_sem("d%d" % i) for i in range(90)]
        self.dma_free = list(self.dma_pool)
        self.all_sems = [e.sem for e in self.eng.values()] + self.dma_pool
        self.uid = 0
        self.rr = 0

    def new_sem(self, name):
        return Sem(self.es.enter_context(self.nc.semaphore(name)), name)

    def sbuf(self, st, name, shape, dt):
        self.uid += 1
        return st.enter_context(self.nc.sbuf_tensor("%s_%d" % (name, self.uid), list(shape), dt))

    def sb(self, st, name, shape, dt):
        t = self.sbuf(st, name, shape, dt)
        return Buf(name, t[:])

    def dram(self, name, shape, dt, kind="Internal"):
        t = self.nc.dram_tensor(name, list(shape), dt, kind=kind)
        return Buf(name, t.ap())

    def _deps(self, reads, writes, pwrites=()):
        deps = {}
        for b in reads:
            for s, v in b.w.items():
                if deps.get(s, 0) < v:
                    deps[s] = v
        for b in writes:
            for s, v in b.w.items():
                if deps.get(s, 0) < v:
                    deps[s] = v
            for s, v in b.r.items():
                if deps.get(s, 0) < v:
                    deps[s] = v
        for b in pwrites:
            for s, v in b.r.items():
                if deps.get(s, 0) < v:
                    deps[s] = v
        return deps

    def _wait(self, e, deps, skip_self=False):
        for s, v in deps.items():
            if skip_self and s is e.sem:
                continue
            if e.waited.get(s, 0) < v:
                e.h.wait_ge(s.h, v)
                e.waited[s] = v

    def op(self, en, fn, ins=(), outs=(), inc=True):
        e = self.eng[en]
        reads = [x.buf if isinstance(x, V) else x for x in ins if x is not None]
        writes = [x.buf if isinstance(x, V) else x for x in outs if x is not None]
        self._wait(e, self._deps(reads, writes), skip_self=(en == "pe"))
        inst = fn(e.h)
        if inc:
            inst.then_inc(e.sem.h, 1)
            e.sem.cnt += 1
            c = e.sem.cnt
        else:
            assert en == "pe"
            c = e.sem.cnt + 1
        for b in reads:
            b.r[e.sem] = c
        for b in writes:
            b.w = {e.sem: c}
            b.r = {}
        return inst

    def dma(self, out, in_, q=None, pw=False, sem_of=None):
        if q is None:
            q = "sp"
        e = self.eng[q]
        rb, wb = in_.buf, out.buf
        if pw:
            deps = self._deps([rb], [], [wb])
        else:
            deps = self._deps([rb], [wb])
        self._wait(e, deps)
        sb = sem_of if sem_of is not None else None
        if sb is None:
            sb = wb if wb.dsem is not None else rb
        assert sb.dsem is not None, (rb.name, wb.name)
        s = sb.dsem
        e.h.dma_start(out=out.ap, in_=in_.ap).then_inc(s.h, 16)
        s.cnt += 16
        rb.r[s] = s.cnt
        if pw:
            wb.w[s] = s.cnt
        else:
            wb.w = {s: s.cnt}
            wb.r = {}

    def give_sem(self, *bufs):
        for b in bufs:
            if b.dsem is None:
                b.dsem = self.dma_free.pop()
        return bufs[0] if len(bufs) == 1 else bufs

    def take_sems(self, bufs):
        for b in bufs:
            if b.dsem is not None:
                self.dma_free.append(b.dsem)
                b.dsem = None

    def barrier(self):
        for e in self.eng.values():
            for s in self.all_sems:
                if s is e.sem:
                    continue
                if s.cnt > 0 and e.waited.get(s, 0) < s.cnt:
                    e.h.wait_ge(s.h, s.cnt)
                    e.waited[s] = s.cnt

    def mm(self, out, lhsT, rhs, start=True, stop=True, extra_in=(), inc=None):
        if inc is None:
            inc = stop
        return self.op("pe", lambda h: h.matmul(out.ap, lhsT=lhsT.ap, rhs=rhs.ap, start=start, stop=stop),
                       ins=[lhsT, rhs] + list(extra_in), outs=[out], inc=inc)

    def transpose(self, out, in_, ident):
        return self.op("pe", lambda h: h.transpose(out.ap, in_.ap, ident.ap), ins=[in_, ident], outs=[out])

    def act(self, out, in_, func, bias=None, scale=None, accum=None, en="act"):
        kw = {}
        ins = [in_]
        if bias is not None:
            if isinstance(bias, V):
                kw["bias"] = bias.ap
                ins.append(bias)
            else:
                kw["bias"] = bias
        if scale is not None:
            if isinstance(scale, V):
                kw["scale"] = scale.ap
                ins.append(scale)
            else:
                kw["scale"] = scale
        outs = [out]
        if accum is not None:
            kw["accum_out"] = accum.ap
            outs.append(accum)
        return self.op("act", lambda h: h.activation(out=out.ap, in_=in_.ap, func=func, **kw), ins=ins, outs=outs)

    def tt(self, out, a, b, op, en="dve"):
        return self.op(en, lambda h: h.tensor_tensor(out=out.ap, in0=a.ap, in1=b.ap, op=op), ins=[a, b], outs=[out])

    def ts(self, out, a, s1, op0, s2=None, op1=None, en="dve"):
        ins = [a]
        kw = {}
        if isinstance(s1, V):
            ins.append(s1)
            s1 = s1.ap
        if isinstance(s2, V):
            ins.append(s2)
            s2 = s2.ap
        if op1 is not None:
            kw["op1"] = op1
        return self.op(en, lambda h: h.tensor_scalar(out=out.ap, in0=a.ap, scalar1=s1, scalar2=s2, op0=op0, **kw),
                       ins=ins, outs=[out])

    def stt(self, out, a, s, b, op0, op1):
        ins = [a, b]
        if isinstance(s, V):
            ins.append(s)
            s = s.ap
        return self.op("dve", lambda h: h.scalar_tensor_tensor(out=out.ap, in0=a.ap, scalar=s, in1=b.ap, op0=op0, op1=op1),
                       ins=ins, outs=[out])

    def copy(self, out, in_, en="dve"):
        if en == "act":
            return self.act(out, in_, AF.Copy)
        return self.op(en, lambda h: h.tensor_copy(out=out.ap, in_=in_.ap), ins=[in_], outs=[out])

    def memset(self, out, val, en="dve"):
        return self.op(en, lambda h: h.memset(out.ap, val), ins=[], outs=[out])

    def recip(self, out, in_):
        return self.op("dve", lambda h: h.reciprocal(out=out.ap, in_=in_.ap), ins=[in_], outs=[out])


class Rot:
    def __init__(self, bufs):
        self.bufs = bufs
        self.i = 0

    def next(self):
        b = self.bufs[self.i % len(self.bufs)]
        self.i += 1
        return b


def sub(ap_tensor_buf, offset_ap):
    return offset_ap


def build(S=4096, NL=4, dbg=(), stop_after=None):
    assert S % 1024 == 0
    NT = S // 128
    NQ = S // 512
    NCMP = (S - 32) // 16 + 1
    NSEL = S // 64
    nc = bass.Bass("TRN2", target_bir_lowering=False)
    k = KB(nc)

    def din(name, shape, dt=F32):
        return k.dram(name, shape, dt, kind="ExternalInput")

    def dscr(name, shape, dt):
        return k.dram(name, shape, dt, kind=("ExternalOutput" if name in dbg else "Internal"))

    x_in = din("xT", [D, S])
    w_in = din("w_in", [NL, D, IN_COLS])
    w_gate = din("w_gate", [NL, D, 3 * D])
    w_out = din("w_out", [NL, D, D])
    wb_fox = din("w_branch_fox", [NL, 512, D])
    wb_diff = din("w_branch_diff", [NL, 512, D])
    wb_nsa = din("w_branch_nsa", [NL, 1024, D])
    w_up = din("w_ffn_up", [NL, D, 2 * DFF])
    w_dn = din("w_ffn_down", [NL, DFF, D])
    cw1 = din("nsa_cmp_w1", [NL, 2, 4096, 256])
    cw2 = din("nsa_cmp_w2", [NL, 2, 256, 128])
    gains = din("gains", [NL, 4, 128, KC])
    fbias = din("fbias", [NL, 4, 1])
    dlam = din("dlam", [NL, 256])
    subln = din("subln", [NL, 128, 1])
    cpos = din("cposT", [NL, 2, 128, 32])
    c_ident = din("c_ident", [128, 128])
    c_ov = din("c_ov", [128, 2, 64])
    c_ekt = din("c_ekt", [64, NT, 128])
    c_lt = din("c_lt", [3, 48, 128])
    c_rt = din("c_rt", [3, 8, 512])
    c_oneh = din("c_oneh", [4, 4, 128])
    c_ka = din("c_ka", [3, S])
    c_qa = din("c_qa", [3, S])
    c_selab = din("c_selab", [3, 2, 2, 4])
    xres_t = nc.dram_tensor("yT", [D, S], F32, kind="ExternalOutput")
    xres = [Buf("xres%d" % c, xres_t.ap()[c * 128:(c + 1) * 128, :]) for c in range(KC)]
    xin = [Buf("xin%d" % c, x_in.ap[c * 128:(c + 1) * 128, :]) for c in range(KC)]

    hT = dscr("hT", [KC, 128, S], BF16)
    qkT = dscr("qkT", [32, 128, S], BF16)
    vtok = dscr("vtok", [NT, 128, 1536], BF16)
    oT = dscr("oT", [KC, 128, S], BF16)
    yscr = dscr("yscr", [KC, 128, S], F32)
    W16 = {}

    WLAY = {}

    def w16(name, K_, N_, lay=None):
        if lay is None:
            W16[name] = [dscr("%s16_%d" % (name, l), [128, K_ // 128, N_], BF16) for l in range(NL)]
        else:
            cws, ks = lay
            WLAY[name] = lay
            W16[name] = [dscr("%s16_%d" % (name, l), [N_ // cws, (K_ // 128) // ks, 128, ks, cws], BF16) for l in range(NL)]

    def wslab(name, l, kc0, kcn, n0, w):
        cws, ks = WLAY[name]
        buf = W16[name][l]
        assert n0 // cws == (n0 + w - 1) // cws and kc0 // ks == (kc0 + kcn - 1) // ks
        return buf.v(buf.ap[n0 // cws, kc0 // ks, :, kc0 % ks:kc0 % ks + kcn, n0 % cws:n0 % cws + w])

    def wstore_pieces(name, dst, k0, kg, n0, cw):
        if name not in WLAY:
            return [(dst.v(dst.ap[:, k0:k0 + kg, n0:n0 + cw]), (0, kg, 0, cw))]
        cws, ks = WLAY[name]
        out = []
        n = n0
        while n < n0 + cw:
            n_hi = min(n0 + cw, (n // cws + 1) * cws)
            kc = k0
            while kc < k0 + kg:
                kc_hi = min(k0 + kg, (kc // ks + 1) * ks)
                out.append((dst.v(dst.ap[n // cws, kc // ks, :, kc % ks:kc % ks + (kc_hi - kc), n % cws:n % cws + (n_hi - n)]),
                            (kc - k0, kc_hi - k0, n - n0, n_hi - n0)))
                kc = kc_hi
            n = n_hi
        return out

    w16("w_in", D, IN_COLS)
    w16("w_gate", D, 3 * D, (512, 16))
    w16("w_out", D, D, (512, 16))
    w16("wb_fox", 512, D, (512, 4))
    w16("wb_diff", 512, D, (512, 4))
    w16("wb_nsa", 1024, D, (512, 8))
    w16("w_up", D, 2 * DFF, (512, 16))
    w16("w_dn", DFF, D, (256, 22))
    w16("cw1k", 4096, 256)
    w16("cw1v", 4096, 256)
    w16("cw2k", 256, 128)
    w16("cw2v", 256, 128)

    gs = ExitStack()
    PS = [Buf("ps%d" % i, gs.enter_context(nc.psum_tensor("ps%d" % i, [128, 512], F32))[:]) for i in range(7)]
    PSB = Buf("psb", gs.enter_context(nc.psum_tensor("psb", [128, 1024], BF16))[:])
    ident_f = k.give_sem(k.sb(gs, "ident_f", [128, 128], F32))
    ident = k.sb(gs, "ident", [128, 128], BF16)
    ones_b = k.sb(gs, "ones_b", [128, 128], BF16)
    ones_f = k.sb(gs, "ones_f", [128, 128], F32)
    sel0_f = k.sb(gs, "sel0_f", [128, 128], F32)
    small_st = k.give_sem(k.sb(gs, "small_st", [128, 260], F32))
    k.dma(ident_f.all, c_ident.all)
    k.copy(ident.all, ident_f.all)
    k.memset(ones_b.all, 1.0)
    k.memset(ones_f.all, 1.0)
    k.memset(sel0_f.all, 0.0)
    k.memset(sel0_f[0:1, :], 1.0)

    KGC = 4
    CWC = 512

    class Caster:
        def __init__(self):
            self.jobs = []
            self.done = set()
            self.cur = None
            self.stg = None
            self.stb = None
            self.engs = ("dve",)
            self.alt = 0

        def alloc(self, st_):
            self.stg = Rot([k.give_sem(k.sb(st_, "cst_f%d" % i, [128, KGC, CWC], F32)) for i in range(2)])
            self.stb = Rot([k.give_sem(k.sb(st_, "cst_b%d" % i, [128, KGC, CWC], BF16)) for i in range(2)])

        def release(self):
            k.take_sems(self.stg.bufs + self.stb.bufs)
            self.stg = None
            self.stb = None

        def add(self, name, src_ap, K_, N_, dst, gain=None):
            self.jobs.append((name, self._gen(src_ap, K_, N_, dst, gain)))

        def _compute_store(self, pend, dst, gain):
            (n0, cw, k0, kg, f, b) = pend
            for j in range(kg):
                en = self.engs[self.alt % len(self.engs)]
                self.alt += 1
                o_ = b.v(b.ap[:, j, 0:cw])
                i_ = f.v(f.ap[:, j, 0:cw])
                g = None if gain is None else gain.buf.v(gain.ap[:, k0 + j:k0 + j + 1])
                if en == "act":
                    k.act(o_, i_, AF.Copy, scale=g)
                elif g is None:
                    k.copy(o_, i_, en=en)
                else:
                    k.ts(o_, i_, g, ALU.mult, en=en)
            for (dv, (ka, kb_, na, nb_)) in wstore_pieces(self.curname, dst, k0, kg, n0, cw):
                k.dma(dv, b.v(b.ap[:, ka:kb_, na:nb_]), q="pool", pw=True)

        def _gen(self, src_ap, K_, N_, dst, gain):
            kcs = K_ // 128
            src = Buf("wsrc", src_ap)
            srcv = src_ap.rearrange("(kc p) n -> p kc n", p=128)
            pend = None
            for n0 in range(0, N_, CWC):
                cw = min(CWC, N_ - n0)
                for k0 in range(0, kcs, KGC):
                    kg = min(KGC, kcs - k0)
                    f = self.stg.next()
                    b = self.stb.next()
                    k.dma(f.v(f.ap[:, 0:kg, 0:cw]), src.v(srcv[:, k0:k0 + kg, n0:n0 + cw]), q="sp")
                    if pend is not None:
                        self._compute_store(pend, dst, gain)
                        yield
                    pend = (n0, cw, k0, kg, f, b)
            self._compute_store(pend, dst, gain)
            yield

        def step(self):
            while True:
                if self.cur is None:
                    if not self.jobs:
                        return False
                    self.cur = self.jobs.pop(0)
                try:
                    self.curname = self.cur[0][0]
                    next(self.cur[1])
                    return True
                except StopIteration:
                    self.done.add(self.cur[0])
                    self.cur = None

        def finish(self, names):
            names = set(names)
            while not names <= self.done:
                if not self.step():
                    break
            assert names <= self.done, (names, self.done)

    caster = Caster()

    gn_l = [k.give_sem(k.sb(gs, "gn%d" % l, [128, 4, KC], F32)) for l in range(NL)]
    lamt_l = [k.sb(gs, "lamt%d" % l, [128, 4], F32) for l in range(NL)]
    subg_l = [k.sb(gs, "subg%d" % l, [128, 4], F32) for l in range(NL)]

    def prep_small(l):
        lam_init = 0.8 - 0.6 * math.exp(-0.3 * l)
        lamt = lamt_l[l]
        k.dma(gn_l[l].all, gains[l].buf.v(gains.ap[l].rearrange("i p c -> p i c")))
        k.dma(small_st[:, 0:256], dlam.v(bass.AP(dlam.ap.tensor, l * 256, [[0, 128], [1, 256]])))
        k.tt(small_st[:, 0:64], small_st[:, 0:64], small_st[:, 64:128], ALU.mult)
        k.tt(small_st[:, 128:192], small_st[:, 128:192], small_st[:, 192:256], ALU.mult)
        k.op("dve", lambda h: h.reduce_sum(out=small_st.ap[:, 256:257], in_=small_st.ap[:, 0:64], axis=mybir.AxisListType.X),
             ins=[small_st], outs=[small_st])
        k.op("dve", lambda h: h.reduce_sum(out=small_st.ap[:, 257:258], in_=small_st.ap[:, 128:192], axis=mybir.AxisListType.X),
             ins=[small_st], outs=[small_st])
        k.act(small_st[:, 256:258], small_st[:, 256:258], AF.Exp)
        k.tt(small_st[:, 258:259], small_st[:, 256:257], small_st[:, 257:258], ALU.subtract)
        k.ts(lamt[:, 0:1], small_st[:, 258:259], lam_init, ALU.add)
        k.ts(lamt[:, 1:2], lamt[:, 0:1], -1.0, ALU.mult)
        k.dma(small_st[:, 259:260], subln[l])
        k.ts(lamt[:, 2:3], small_st[:, 259:260], (1.0 - lam_init), ALU.mult)
        for j in range(4):
            k.copy(subg_l[l][:, j:j + 1], lamt[:, 2:3])

    def add_cast_jobs_early(l):
        caster.add(("w_in", l), w_in.ap[l], D, IN_COLS, W16["w_in"][l], gain=gn_l[l][:, 0, :])

    def add_cast_jobs_rest(l):
        caster.add(("cw1k", l), cw1.ap[l, 0], 4096, 256, W16["cw1k"][l])
        caster.add(("cw1v", l), cw1.ap[l, 1], 4096, 256, W16["cw1v"][l])
        caster.add(("cw2k", l), cw2.ap[l, 0], 256, 128, W16["cw2k"][l])
        caster.add(("cw2v", l), cw2.ap[l, 1], 256, 128, W16["cw2v"][l])
        caster.add(("w_gate", l), w_gate.ap[l], D, 3 * D, W16["w_gate"][l], gain=gn_l[l][:, 0, :])
        caster.add(("wb_fox", l), wb_fox.ap[l], 512, D, W16["wb_fox"][l])
        caster.add(("wb_diff", l), wb_diff.ap[l], 512, D, W16["wb_diff"][l], gain=subg_l[l].all)
        caster.add(("wb_nsa", l), wb_nsa.ap[l], 1024, D, W16["wb_nsa"][l])
        caster.add(("w_out", l), w_out.ap[l], D, D, W16["w_out"][l])
        caster.add(("w_up", l), w_up.ap[l], D, 2 * DFF, W16["w_up"][l], gain=gn_l[l][:, 2, :])
        caster.add(("w_dn", l), w_dn.ap[l], DFF, D, W16["w_dn"][l])

    CMP_JOBS = ["cw1k", "cw1v", "cw2k", "cw2v"]
    REST_JOBS = ["w_gate", "wb_fox", "wb_diff", "wb_nsa", "w_out", "w_up", "w_dn"]

    def gen_norm(src, t0, TB, hb, xrot, sqrot, rstd, psA, keep=None, rs_dram=None):
        nsub = TB // 512
        if rs_dram is not None:
            k.dma(rstd.all, rs_dram.v(bass.AP(rs_dram.ap.tensor, t0, [[0, 128], [1, TB]])))
            for c in range(KC):
                xb = xrot.next()
                k.dma(xb.all, src[c][:, t0:t0 + TB])
                k.tt(hb[c].all, xb.all, rstd.all, ALU.mult, en="dve")
                yield
            return
        for c in range(KC):
            xb = keep[c] if keep is not None else xrot.next()
            k.dma(xb.all, src[c][:, t0:t0 + TB])
            sq = sqrot.next()
            if c % 2 == 0:
                k.act(sq.all, xb.all, AF.Square)
            else:
                k.tt(sq.all, xb.all, xb.all, ALU.mult, en="dve")
            for n in range(nsub):
                k.mm(psA[n][:, :], ones_b.all, sq[:, n * 512:(n + 1) * 512], start=(c == 0), stop=(c == KC - 1), inc=True)
            yield
        for n in range(nsub):
            k.act(rstd[:, n * 512:(n + 1) * 512], psA[n][:, :], AF.Sqrt, scale=1.0 / D, bias=EPS)
        k.recip(rstd.all, rstd.all)
        for c in range(KC):
            if keep is not None:
                xb = keep[c]
            else:
                xb = xrot.next()
                k.dma(xb.all, src[c][:, t0:t0 + TB])
            k.tt(hb[c].all, xb.all, rstd.all, ALU.mult, en="dve")
            yield

    def run_gen(g):
        for _ in g:
            pass

    class BG:
        def __init__(self, gens):
            self.gens = list(gens)

        def step(self, n=1):
            for _ in range(n):
                while self.gens:
                    try:
                        next(self.gens[0])
                        break
                    except StopIteration:
                        self.gens.pop(0)

        def drain(self):
            while self.gens:
                self.step()

    FM_SLABS = [
        (C_FQ, 512, [0, 1, 2, 3]), (C_FK, 512, [4, 5, 6, 7]),
        (C_DQ, 512, [8, 9, 10, 11]), (C_DK, 512, [12, 13, 14, 15]),
        (C_NQ, 512, [16, 17, 18, 19]), (C_NQ + 512, 512, [20, 21, 22, 23]),
        (C_NKC, 512, [24, 25, 26, 27]), (C_NKS, 256, [28, 29]), (C_NKW, 256, [30, 31]),
    ]
    TM_SLABS = [(C_FV, 512, 0), (C_DV, 512, 512), (C_NVS, 256, 1024), (C_NVW, 256, 1280)]
    QSCALE = {}
    for t_ in [0, 1, 2, 3] + list(range(16, 24)):
        QSCALE[t_] = 128.0 ** -0.5
    for t_ in [8, 9, 10, 11]:
        QSCALE[t_] = 0.125

    rsd = dscr("rsd", [1, S], F32)
    lfd = dscr("lfd", [4, S], F32)
    lfcarry = k.sb(gs, "lfcarry", [4, 1], F32)
    nsag = k.sb(gs, "nsag", [128, NT, 24], F32)
    negfb = k.give_sem(k.sb(gs, "negfb", [4, 2], F32))

    def phase1(l, src):
        TB = 1024
        st = ExitStack()
        hbs = [[k.give_sem(k.sb(st, "h%d_%d" % (i, c), [128, TB], BF16)) for c in range(KC)] for i in range(2)]
        xrot = Rot([k.give_sem(k.sb(st, "xr%d" % i, [128, TB], F32)) for i in range(1)])
        xkeep = [k.give_sem(k.sb(st, "xk%d" % c, [128, TB], F32)) for c in range(KC)]
        sqrot = Rot([k.sb(st, "sq%d" % i, [128, TB], BF16) for i in range(2)])
        rstd = k.sb(st, "rstd", [128, TB], F32)
        slabs = Rot([k.give_sem(k.sb(st, "slab%d" % i, [128, KC, 512], BF16)) for i in range(2)])
        ostg = Rot([k.give_sem(k.sb(st, "ostg%d" % i, [128, TB], BF16)) for i in range(3)])
        vstg = Rot([k.give_sem(k.sb(st, "vstg%d" % i, [128, 512], BF16)) for i in range(3)])
        ffst = k.sb(st, "ffst", [4, 512], F32)
        lfblk = k.give_sem(k.sb(st, "lfblk", [4, TB], F32))
        w = W16["w_in"][l]
        k.dma(negfb[:, 0:1], fbias[l])
        k.ts(negfb[:, 1:2], negfb[:, 0:1], -1.0, ALU.mult)
        psrot = Rot(PS[2:6])
        ev = 0
        run_gen(gen_norm(src, 0, TB, hbs[0], xrot, sqrot, rstd, PS[0:2], keep=xkeep))
        for tb in range(S // TB):
            t0 = tb * TB
            hb = hbs[tb % 2]
            bg = BG([gen_norm(src, t0 + TB, TB, hbs[(tb + 1) % 2], xrot, sqrot, rstd, PS[0:2], keep=xkeep)] if tb + 1 < S // TB else [])
            for c in range(KC):
                k.dma(hT.v(hT.ap[c, :, t0:t0 + TB]), hb[c].all, pw=True)
            for (c0, wd, tiles) in FM_SLABS:
                sl = slabs.next()
                k.dma(sl.v(sl.ap[:, :, 0:wd]), w.v(w.ap[:, :, c0:c0 + wd]))
                for mi, tid in enumerate(tiles):
                    og = ostg.next()
                    for n in range(TB // 512):
                        ps = psrot.next()
                        for c in range(KC):
                            k.mm(ps[:, :], sl.v(sl.ap[:, c, mi * 128:(mi + 1) * 128]), hb[c][:, n * 512:(n + 1) * 512],
                                 start=(c == 0), stop=(c == KC - 1))
                        qs = QSCALE.get(tid, 1.0)
                        if ev % 2 == 0:
                            k.act(og[:, n * 512:(n + 1) * 512], ps[:, :], AF.Copy, scale=qs)
                        else:
                            k.ts(og[:, n * 512:(n + 1) * 512], ps[:, :], qs, ALU.mult)
                        ev += 1
                    k.dma(qkT.v(qkT.ap[tid, :, t0:t0 + TB]), og.all, pw=True)
                    bg.step(2)
            bg.drain()
            for (c0, wd, vc0) in TM_SLABS:
                sl = slabs.next()
                k.dma(sl.v(sl.ap[:, :, 0:wd]), w.v(w.ap[:, :, c0:c0 + wd]))
                for tt_ in range(TB // 128):
                    ps = psrot.next()
                    for c in range(KC):
                        k.mm(ps[:, 0:wd], hb[c][:, tt_ * 128:(tt_ + 1) * 128], sl.v(sl.ap[:, c, 0:wd]),
                             start=(c == 0), stop=(c == KC - 1))
                    vg = vstg.next()
                    if ev % 2 == 0:
                        k.act(vg[:, 0:wd], ps[:, 0:wd], AF.Copy)
                    else:
                        k.copy(vg[:, 0:wd], ps[:, 0:wd])
                    ev += 1
                    k.dma(vtok.v(vtok.ap[t0 // 128 + tt_, :, vc0:vc0 + wd]), vg[:, 0:wd], pw=True)
            sl = slabs.next()
            k.dma(sl.v(sl.ap[:, :, 0:4]), w.v(w.ap[:, :, C_FF:C_FF + 4]))
            k.dma(sl.v(sl.ap[:, :, 8:32]), w.v(w.ap[:, :, C_NG:C_NG + 24]))
            for n in range(TB // 512):
                ps = psrot.next()
                for c in range(KC):
                    k.mm(ps[0:4, :], sl.v(sl.ap[:, c, 0:4]), hb[c][:, n * 512:(n + 1) * 512], start=(c == 0), stop=(c == KC - 1))
                k.act(ffst[:, :], ps[0:4, :], AF.Exp, scale=-1.0, bias=negfb[:, 1:2])
                k.act(ffst[:, :], ffst[:, :], AF.Ln, bias=1.0)
                k.ts(ffst[:, :], ffst[:, :], -1.0, ALU.mult)
                a0 = n * 512
                if n > 0:
                    init = lfblk.ap[:, a0 - 1:a0]
                    init_ins = []
                elif tb > 0:
                    init = lfcarry.ap[:, 0:1]
                    init_ins = [lfcarry]
                else:
                    init = 0.0
                    init_ins = []
                k.op("dve", lambda h, a0=a0, init=init: h.tensor_tensor_scan(
                    out=lfblk.ap[:, a0:a0 + 512], data0=onesrow.ap[0:4, :], data1=ffst.ap[:, :],
                    initial=init, op0=ALU.mult, op1=ALU.add), ins=[ffst, onesrow, lfblk] + init_ins, outs=[lfblk])
            k.copy(lfcarry.all, lfblk[:, TB - 1:TB])
            k.dma(lfd[:, t0:t0 + TB], lfblk.all, pw=True)
            for tt_ in range(TB // 128):
                ps = psrot.next()
                for c in range(KC):
                    k.mm(ps[:, 0:24], hb[c][:, tt_ * 128:(tt_ + 1) * 128], sl.v(sl.ap[:, c, 8:32]), start=(c == 0), stop=(c == KC - 1))
                k.act(nsag[:, t0 // 128 + tt_, :], ps[:, 0:24], AF.Sigmoid)
        k.barrier()
        k.take_sems(hbs[0] + hbs[1] + xrot.bufs + xkeep + slabs.bufs + ostg.bufs + vstg.bufs + [lfblk])
        st.close()

    onesrow = k.sb(gs, "onesrow", [4, 512], F32)
    k.memset(onesrow.all, 1.0)

    class Pipe:
        def __init__(self, depth=2, bg=None, bg_every=4):
            self.q = []
            self.deferred = []
            self.depth = depth
            self.bg = bg
            self.bg_every = bg_every
            self.n = 0

        def push(self, t):
            t["qk"]()
            self.q.append(t)
            if len(self.q) > self.depth:
                self._finish(self.q.pop(0))

        def _finish(self, t):
            t["sm"]()
            t["pv"]()
            for fn in t.get("post", ()):
                fn()
            for d in self.deferred:
                d[0] -= 1
            ready = [d for d in self.deferred if d[0] <= 0]
            self.deferred = [d for d in self.deferred if d[0] > 0]
            for d in ready:
                d[1]()
            self.n += 1
            if self.bg is not None and self.n % self.bg_every == 0:
                self.bg()

        def defer(self, n, fn):
            self.deferred.append([n, fn])

        def flush(self):
            while self.q:
                self._finish(self.q.pop(0))
            for d in self.deferred:
                d[1]()
            self.deferred = []

    def att_tile(pipe, strot, prot, nk, col0, qk_list, aux_list, bias, masks, pv_list, post=()):
        ps = strot.next()
        pt = prot.next()
        mms = list(qk_list) + list(aux_list)

        def qk():
            for i, (lt, rh) in enumerate(mms):
                k.mm(ps[0:nk, col0:512], lt, rh, start=(i == 0), stop=(i == len(mms) - 1))

        def sm():
            k.act(pt[0:nk, col0:512], ps[0:nk, col0:512], AF.Exp, bias=bias)
            for (c_lo, c_hi, pattern, base, cm) in masks:
                k.op("pool", lambda h, c_lo=c_lo, c_hi=c_hi, pattern=pattern, base=base, cm=cm: h.affine_select(
                    out=pt.ap[0:nk, c_lo:c_hi], in_=pt.ap[0:nk, c_lo:c_hi], pattern=pattern, compare_op=ALU.is_ge,
                    fill=0.0, base=base, channel_multiplier=cm), ins=[pt], outs=[pt])

        def pv():
            for i_, (o, c_lo, c_hi, rh, start, stop) in enumerate(pv_list):
                k.mm(o, pt[0:nk, c_lo:c_hi], rh, start=start, stop=stop, inc=(i_ == len(pv_list) - 1))

        pipe.push({"qk": qk, "sm": sm, "pv": pv, "post": list(post)})

    def oset_views(banks, w):
        return [banks[j // 2].v(banks[j // 2].ap[:, (j % 2) * w:(j % 2) * w + w]) for j in range(4)]

    def phase2(l):
        st = ExitStack()
        caster.alloc(st)
        pipe = Pipe(depth=2, bg=caster.step, bg_every=8)
        strot = Rot(PS[0:3])
        prot = Rot([k.sb(st, "pt%d" % i, [128, 512], BF16) for i in range(3)])
        osets = [PS[3:5], PS[5:7]]
        denrot = Rot([k.sb(st, "den%d" % i, [128, 8], F32) for i in range(4)])
        obrot = Rot([k.sb(st, "ob%d" % i, [128, 4, 128], BF16) for i in range(2)])
        otrot = Rot([k.give_sem(k.sb(st, "ot%d" % i, [128, 512], BF16)) for i in range(2)])
        tfrot = Rot([k.sb(st, "tf%d" % i, [128, 4, 128], F32) for i in range(2)])
        psb_half = [0]

        LTa = k.sb(st, "LTa", [128, 48, 128], BF16)
        RTa = k.sb(st, "RTa", [128, 8, 512], BF16)
        EKT = k.sb(st, "EKT", [128, NT, 128], BF16)
        SAB = k.sb(st, "SAB", [128, 2, 2, 4], F32)
        hst = ExitStack()
        lf = k.give_sem(k.sb(hst, "lf", [4, S], F32))
        k.dma(lf.all, lfd.all)
        negcum = k.sb(hst, "negcum", [128, NT, 4], F32)
        Aall = k.sb(hst, "Aall", [128, S], BF16)
        k.memset(Aall.all, 0.0)
        CA = k.sb(hst, "CA", [128, S], BF16)
        CB = k.sb(hst, "CB", [128, S], BF16)
        ONEH = k.sb(hst, "ONEH", [128, 4, 128], BF16)
        cstk = ExitStack()
        cst = k.give_sem(k.sb(cstk, "cst_f", [128, max(S, 6144)], F32))
        for tbuf in (LTa, RTa, ONEH, EKT, CA, CB, SAB):
            k.memset(tbuf.all, 0.0)
        fl = "r a b -> r (a b)"
        k.dma(cst.v(cst.ap[0:3, 0:48 * 128]), c_lt.v(c_lt.ap.rearrange(fl)))
        k.copy(LTa.v(LTa.ap[0:3].rearrange(fl)), cst.v(cst.ap[0:3, 0:48 * 128]))
        k.dma(cst.v(cst.ap[0:3, 0:8 * 512]), c_rt.v(c_rt.ap.rearrange(fl)))
        k.copy(RTa.v(RTa.ap[0:3].rearrange(fl)), cst.v(cst.ap[0:3, 0:8 * 512]))
        k.dma(cst.v(cst.ap[0:4, 0:512]), c_oneh.v(c_oneh.ap.rearrange(fl)))
        k.copy(ONEH.v(ONEH.ap[0:4].rearrange(fl)), cst.v(cst.ap[0:4, 0:512]))
        k.dma(cst.v(cst.ap[0:64, 0:S]), c_ekt.v(c_ekt.ap.rearrange(fl)))
        k.dma(cst.v(cst.ap[64:67, 0:S]), c_ka.all)
        k.copy(EKT.v(EKT.ap[0:64].rearrange(fl)), cst.v(cst.ap[0:64, 0:S]))
        k.copy(EKT.v(EKT.ap[64:67].rearrange(fl)), cst.v(cst.ap[64:67, 0:S]))
        k.copy(CA[64:67, :], cst.v(cst.ap[64:67, 0:S]))
        k.dma(cst.v(cst.ap[0:3, 0:S]), c_ka.all)
        k.copy(CA[0:3, :], cst.v(cst.ap[0:3, 0:S]))
        k.dma(cst.v(cst.ap[0:3, 0:S]), c_qa.all)
        k.copy(CB[0:3, :], cst.v(cst.ap[0:3, 0:S]))
        k.dma(cst.v(cst.ap[64:67, 0:S]), c_qa.all)
        k.copy(CB[64:67, :], cst.v(cst.ap[64:67, 0:S]))
        k.dma(cst.v(cst.ap[64:67, 0:16]), c_selab.v(c_selab.ap.rearrange("r a b c -> r (a b c)")))
        k.copy(SAB.v(SAB.ap[64:67].rearrange("r a b c -> r (a b c)")), cst.v(cst.ap[64:67, 0:16]))
        k.barrier()
        k.take_sems([cst])
        cstk.close()

        for kt in range(NT):
            k.transpose(PS[0][:, kt * 4:(kt + 1) * 4], lf[0:4, kt * 128:(kt + 1) * 128], ident_f[0:4, 0:4])
        k.ts(negcum.v(negcum.ap.rearrange("p a b -> p (a b)")), PS[0][:, 0:NT * 4], -1.0, ALU.mult)
        lf3 = lf.ap.rearrange("p (a b) -> p a b", b=128)
        k.op("dve", lambda h: h.tensor_copy(out=Aall.ap[0:4].rearrange("p (a b) -> p a b", b=128),
                                            in_=bass.AP(lf3.tensor, lf3.offset, [list(lf3.ap[0]), list(lf3.ap[1]), [0, 128]])),
             ins=[lf], outs=[Aall])

        def fin_transposes(ob, dst_views):
            half = psb_half[0] % 2
            psb_half[0] += 1
            for j in range(4):
                k.transpose(PSB[:, half * 512 + j * 128: half * 512 + (j + 1) * 128], ob[:, j, :], ident.all)
            ot = otrot.next()
            k.copy(ot.all, PSB[:, half * 512:(half + 1) * 512])
            if len(dst_views) == 1:
                k.dma(dst_views[0], ot.all, pw=True)
            else:
                for j in range(4):
                    k.dma(dst_views[j], ot[:, j * 128:(j + 1) * 128], pw=True)

        qrot = Rot([k.give_sem(k.sb(hst, "qT%d" % i, [128, S], BF16)) for i in range(2)])
        krot = Rot([k.give_sem(k.sb(hst, "kT%d" % i, [128, S], BF16)) for i in range(2)])
        vrot = Rot([k.give_sem(k.sb(hst, "vh%d" % i, [128, NT, 129], BF16)) for i in range(2)])
        for vb in vrot.bufs:
            k.memset(vb.all, 1.0)

        def load_head(qtile, ktile, vcol):
            qb, kb, vb = qrot.next(), krot.next(), vrot.next()
            k.dma(qb.all, qkT[qtile])
            k.dma(kb.all, qkT[ktile])
            k.dma(vb.v(vb.ap[:, :, 0:128]), vtok.v(vtok.ap[:, :, vcol:vcol + 128].rearrange("t p c -> p t c")))
            return qb, kb, vb

        gcount = [0]
        for h in range(4):
            qb, kb, vb = load_head(h, 4 + h, h * 128)
            for qt in range(NQ):
                oset = oset_views(osets[gcount[0] % 2], 129)
                gcount[0] += 1
                started = [False, False]
                nkt = 4 * qt + 4
                for kt in range(nkt):
                    m = kt - 4 * qt
                    j0 = max(m, 0)
                    col0 = 128 * j0
                    masks = []
                    if m >= 0:
                        masks.append((128 * m, 128 * m + 128, [[1, 128]], 0, -1))
                    pvl = []
                    for j in range(j0, 4):
                        bnk = j // 2
                        pvl.append((oset[j], 128 * j, 128 * j + 128, vb.v(vb.ap[:, kt, :]), not started[bnk], kt == nkt - 1))
                        started[bnk] = True
                    post = []
                    if kt == nkt - 1:
                        def fin(h=h, qt=qt, oset=oset):
                            den = denrot.next()
                            ob = obrot.next()
                            for j in range(4):
                                k.ts(den[:, j:j + 1], oset[j][:, 128:129], 1e-30, ALU.max)
                                k.recip(den[:, j:j + 1], den[:, j:j + 1])
                                k.act(ob[:, j, :], oset[j][:, 0:128], AF.Copy, scale=den[:, j:j + 1])
                            pipe.defer(2, lambda: fin_transposes(ob, [oT.v(oT.ap[h, :, qt * 512:(qt + 1) * 512])]))
                        post.append(fin)
                    att_tile(pipe, strot, prot, 128, col0,
                             [(kb[:, kt * 128:(kt + 1) * 128], qb[:, qt * 512 + col0:(qt + 1) * 512])],
                             [(ONEH[:, h, :], Aall[:, qt * 512 + col0:(qt + 1) * 512])],
                             negcum[:, kt, h:h + 1], masks, pvl, post)
        pipe.flush()
        kz = [krot.bufs[0], krot.bufs[1]]
        qz = [qrot.bufs[0], qrot.bufs[1]]
        k.copy(kz[0][64:128, :], CA[64:128, :])
        k.copy(kz[1][0:64, :], CA[0:64, :])
        k.memset(qz[0][64:128, :], 0.0)
        k.memset(qz[1][0:64, :], 0.0)
        for h in range(4):
            slope_h = 2.0 ** (-2.0 * (h + 1))
            vb = vrot.next()
            k.dma(kz[0][0:64, :], qkT[12 + h][0:64, :])
            k.dma(kz[1][64:128, :], qkT[12 + h][64:128, :])
            k.dma(qz[0][0:64, :], qkT[8 + h][0:64, :])
            k.dma(qz[1][64:128, :], qkT[8 + h][64:128, :])
            k.ts(qz[0][64:67, :], CB[64:67, :], slope_h, ALU.mult)
            k.ts(qz[1][0:3, :], CB[0:3, :], slope_h, ALU.mult)
            k.dma(vb.v(vb.ap[:, :, 0:128]), vtok.v(vtok.ap[:, :, 512 + h * 128:512 + (h + 1) * 128].rearrange("t p c -> p t c")))
            for qt in range(NQ):
                nkt = 4 * qt + 4
                osm = [oset_views(osets[0], 129), oset_views(osets[1], 129)]
                for mp in range(2):
                    oset = osm[mp]
                    started = [False, False]
                    r0 = 64 * mp
                    for kt in range(nkt):
                        m = kt - 4 * qt
                        j0 = max(m, 0)
                        col0 = 128 * j0
                        masks = []
                        if m >= 0:
                            masks.append((128 * m, 128 * m + 128, [[1, 128]], 0, -1))
                        pvl = []
                        for j in range(j0, 4):
                            bnk = j // 2
                            pvl.append((oset[j], 128 * j, 128 * j + 128, vb.v(vb.ap[:, kt, :]), not started[bnk], kt == nkt - 1))
                            started[bnk] = True
                        post = []
                        if kt == nkt - 1 and mp == 1:
                            def fin(h=h, qt=qt, osm=osm):
                                den = denrot.next()
                                ob = obrot.next()
                                tf = tfrot.next()
                                for j in range(4):
                                    for mp_ in range(2):
                                        k.ts(den[:, 2 * j + mp_:2 * j + mp_ + 1], osm[mp_][j][:, 128:129], 1e-30, ALU.max)
                                    k.recip(den[:, 2 * j:2 * j + 2], den[:, 2 * j:2 * j + 2])
                                    k.tt(den[:, 2 * j + 1:2 * j + 2], den[:, 2 * j + 1:2 * j + 2], lamt_l[l][:, 1:2], ALU.mult)
                                    k.act(tf[:, j, :], osm[0][j][:, 0:128], AF.Copy, scale=den[:, 2 * j:2 * j + 1])
                                    k.stt(tf[:, j, :], osm[1][j][:, 0:128], den[:, 2 * j + 1:2 * j + 2], tf[:, j, :], ALU.mult, ALU.add)
                                den2 = denrot.next()
                                sq = tfrot.next()
                                for j in range(4):
                                    k.act(sq[:, j, :], tf[:, j, :], AF.Square, accum=den2[:, j:j + 1])
                                k.act(den2[:, 0:4], den2[:, 0:4], AF.Sqrt, scale=1.0 / 128.0, bias=EPS)
                                k.recip(den2[:, 0:4], den2[:, 0:4])
                                for j in range(4):
                                    k.ts(ob[:, j, :], tf[:, j, :], den2[:, j:j + 1], ALU.mult)
                                pipe.defer(2, lambda: fin_transposes(ob, [oT.v(oT.ap[4 + h, :, qt * 512:(qt + 1) * 512])]))
                            post.append(fin)
                        att_tile(pipe, strot, prot, 128, col0,
                                 [(kz[mp][:, kt * 128:(kt + 1) * 128], qz[mp][:, qt * 512 + col0:(qt + 1) * 512])],
                                 [],
                                 None, masks, pvl, post)
        pipe.flush()
        k.barrier()
        k.take_sems(qrot.bufs + krot.bufs + vrot.bufs + [lf])
        hst.close()

        nst = ExitStack()
        q4 = k.give_sem(k.sb(nst, "q4", [128, 4, S], BF16))
        ksT = k.give_sem(k.sb(nst, "ksT", [128, S], BF16))
        kwT = k.give_sem(k.sb(nst, "kwT", [128, S], BF16))
        vs = k.give_sem(k.sb(nst, "vs", [128, NT, 129], BF16))
        vw = k.give_sem(k.sb(nst, "vw", [128, NT, 129], BF16))
        kcT2 = k.sb(nst, "kcT2", [128, 2, 256], BF16)
        vcm2 = k.sb(nst, "vcm2", [128, 2, 2, 193], BF16)
        ovf = k.give_sem(k.sb(nst, "ovf", [128, 2, 64], F32))
        accrot = Rot([k.sb(nst, "acc%d" % i, [128, 4, 128], F32) for i in range(2)])
        imp = k.sb(nst, "imp", [128, 64], F32)
        imp2 = k.sb(nst, "imp2", [128, 64], F32)
        mx = k.sb(nst, "mx", [128, 16], F32)
        selb = k.sb(nst, "selb", [128, 64], BF16)
        selT4 = Rot([k.sb(nst, "selT4_%d" % i, [128, 4, 128], BF16) for i in range(2)])
        gsc = Rot([k.sb(nst, "gsc%d" % i, [128, 4], F32) for i in range(4)])
        for sb_ in selT4.bufs:
            k.memset(sb_.all, 0.0)
        k.memset(vs.all, 1.0)
        k.memset(vw.all, 1.0)
        k.memset(vcm2.all, 1.0)
        k.dma(ovf.all, c_ov.all)
        caster.finish([(n_, l) for n_ in CMP_JOBS])
        cstk2 = ExitStack()
        ncT = k.give_sem(k.sb(cstk2, "ncT", [128, S], BF16))
        Xc = k.sb(cstk2, "Xc", [128, 32, 256], BF16)
        w1s = k.give_sem(k.sb(cstk2, "w1s", [128, 32, 256], BF16))
        w2s = k.give_sem(k.sb(cstk2, "w2s", [128, 2, 128], BF16))
        hcT = k.sb(cstk2, "hcT", [128, 2, 256], BF16)
        posT = k.give_sem(k.sb(cstk2, "posT", [128, 2, 32], F32))
        k.memset(Xc.all, 0.0)
        k.dma(posT.all, cpos.v(cpos.ap[l].rearrange("a d b -> d a b")))

        def compress(g, kv):
            k.dma(ncT.all, qkT[24 + 2 * kv + g])
            k.dma(w1s.all, W16["cw1k" if kv == 0 else "cw1v"][l].all)
            k.dma(w2s.all, W16["cw2k" if kv == 0 else "cw2v"][l].all)
            nc3 = ncT.ap.rearrange("p (c s) -> p c s", s=16)
            for li in range(32):
                a_, b_ = li // 16, li % 16
                src = nc3[:, a_:a_ + NCMP, b_]
                k.ts(Xc.v(Xc.ap[:, li, 0:NCMP]), ncT.v(src), posT[:, kv, li:li + 1], ALU.add, en="dve")
            for m in range(2):
                ps = PS[m]
                for li in range(32):
                    k.mm(ps[:, 0:256], w1s.v(w1s.ap[:, li, m * 128:(m + 1) * 128]), Xc.v(Xc.ap[:, li, :]), start=(li == 0), stop=(li == 31))
                k.act(hcT[:, m, :], ps[:, 0:256], AF.Silu)
            if kv == 0:
                for m in range(2):
                    k.mm(PS[2][:, 0:256], w2s.v(w2s.ap[:, m, :]), hcT[:, m, :], start=(m == 0), stop=(m == 1))
                k.copy(kcT2[:, g, :], PS[2][:, 0:256])
            else:
                for ct in range(2):
                    for m in range(2):
                        k.mm(PS[2][:, ct * 128:(ct + 1) * 128], hcT[:, m, ct * 128:(ct + 1) * 128], w2s.v(w2s.ap[:, m, :]),
                             start=(m == 0), stop=(m == 1))
                    k.copy(vcm2[:, g, ct, 0:128], PS[2][:, ct * 128:(ct + 1) * 128])

        for g in range(2):
            compress(g, 0)
            compress(g, 1)
            k.copy(vcm2[:, g, :, 129:193], ovf.all)
        k.barrier()
        k.take_sems([ncT, w1s, w2s, posT])
        cstk2.close()

        for g in range(2):
            k.dma(q4.all, qkT.v(qkT.ap[16 + 4 * g:20 + 4 * g].rearrange("j p s -> p j s")))
            k.dma(ksT.all, qkT[28 + g])
            k.dma(kwT.all, qkT[30 + g])
            k.dma(vs.v(vs.ap[:, :, 0:128]), vtok.v(vtok.ap[:, :, 1024 + g * 128:1024 + (g + 1) * 128].rearrange("t p c -> p t c")))
            k.dma(vw.v(vw.ap[:, :, 0:128]), vtok.v(vtok.ap[:, :, 1280 + g * 128:1280 + (g + 1) * 128].rearrange("t p c -> p t c")))
            pending_sel = [None]
            for jt in range(NT):
                acc = accrot.next()
                sT = selT4.next()
                def bc(ap_):
                    return bass.AP(ap_.tensor, ap_.offset, [list(ap_.ap[0]), list(ap_.ap[1]), [0, 128]])
                k.stt(sT.v(sT.ap[64:67]), SAB.v(bc(SAB.ap[64:67, g, 1, :])), float(jt), SAB.v(bc(SAB.ap[64:67, g, 0, :])), ALU.mult, ALU.add)
                qcols = q4.v(q4.ap[:, :, jt * 128:(jt + 1) * 128])
                full4 = [[0, 4], [1, 128]]

                def gate_scaled(oset, w, br, first, jt=jt, g=g, acc=acc):
                    den = denrot.next()
                    gs_ = gsc.next()
                    for j in range(4):
                        k.ts(den[:, j:j + 1], oset[j][:, 128:129], 1e-30, ALU.max)
                    k.recip(den[:, 0:4], den[:, 0:4])
                    for j in range(4):
                        col = (4 * g + j) * 3 + br
                        k.tt(gs_[:, j:j + 1], den[:, j:j + 1], nsag[:, jt, col:col + 1], ALU.mult)
                        if first:
                            k.act(acc[:, j, :], oset[j][:, 0:128], AF.Copy, scale=gs_[:, j:j + 1])
                        else:
                            k.stt(acc[:, j, :], oset[j][:, 0:128], gs_[:, j:j + 1], acc[:, j, :], ALU.mult, ALU.add)
                    return den

                oset = oset_views(osets[0], 193)
                started = [False, False]
                cts = [(0, 128)]
                if jt >= 16:
                    cts.append((1, 128))
                for ci, (ct, nk) in enumerate(cts):
                    last = ci == len(cts) - 1
                    pvl = []
                    for j in range(4):
                        bnk = j // 2
                        pvl.append((oset[j], 128 * j, 128 * j + 128, vcm2.v(vcm2.ap[:, g, ct, :]), not started[bnk], last))
                        started[bnk] = True
                    post = []
                    if last:
                        def fin_cmp(oset=oset, jt=jt, sT=sT, gate_scaled=gate_scaled):
                            den = gate_scaled(oset, 193, 0, True)
                            for j in range(4):
                                if j == 0:
                                    k.ts(imp.all, oset[j][:, 129:193], den[:, 0:1], ALU.mult)
                                else:
                                    k.stt(imp.all, oset[j][:, 129:193], den[:, j:j + 1], imp.all, ALU.mult, ALU.add)
                            for hf in range(2):
                                cur = 2 * jt + hf
                                r0, r1 = 64 * hf, 64 * hf + 64
                                if cur + 1 < 64:
                                    k.memset(imp[r0:r1, cur + 1:64], -1.0)
                                k.memset(imp[r0:r1, 0:1], 1e4)
                                k.memset(imp[r0:r1, max(cur - 1, 0):cur + 1], 1e4)
                            k.op("dve", lambda h: h.max(out=mx.ap[:, 0:8], in_=imp.ap), ins=[imp], outs=[mx])
                            k.op("dve", lambda h: h.match_replace(out=imp2.ap, in_to_replace=mx.ap[:, 0:8], in_values=imp.ap, imm_value=-1e30),
                                 ins=[imp, mx], outs=[imp2])
                            k.op("dve", lambda h: h.max(out=mx.ap[:, 8:16], in_=imp2.ap), ins=[imp2, mx], outs=[mx])
                            k.ts(imp2.all, imp.all, mx[:, 15:16], ALU.is_ge)
                            k.ts(selb.all, imp2.all, 1.0, ALU.subtract, 30000.0, ALU.mult)

                            def tr():
                                half = psb_half[0] % 2
                                psb_half[0] += 1
                                k.transpose(PSB[0:64, half * 512:half * 512 + 128], selb.all, ident.all)
                                for j in range(4):
                                    k.copy(sT[0:64, j, :], PSB[0:64, half * 512:half * 512 + 128])
                            pipe.defer(1, tr)
                        post.append(fin_cmp)
                    dd = 16 * ct - jt + 31
                    att_tile(pipe, strot, prot, nk, 0,
                             [(kcT2[:, g, ct * 128:(ct + 1) * 128], qcols)],
                             [(LTa[:, dd, :], RTa[:, 6 + g, :])],
                             None, [(0, 512, full4, 128 * jt - 31 - 2048 * ct, -16)], pvl, post)
                oset = oset_views(osets[1], 129)
                started = [False, False]
                kts = list(range(max(0, jt - 4), jt + 1))
                for kt in kts:
                    last = kt == jt
                    masks = []
                    if kt == jt:
                        masks.append((0, 512, full4, 0, -1))
                    if kt == jt - 4:
                        masks.append((0, 512, [[0, 4], [-1, 128]], -1, 1))
                    pvl = []
                    for j in range(4):
                        bnk = j // 2
                        pvl.append((oset[j], 128 * j, 128 * j + 128, vw.v(vw.ap[:, kt, :]), not started[bnk], last))
                        started[bnk] = True
                    post = []
                    if last:
                        def fin_win(oset=oset, gate_scaled=gate_scaled):
                            gate_scaled(oset, 129, 2, False)
                        post.append(fin_win)
                    dd = kt - jt + 31
                    att_tile(pipe, strot, prot, 128, 0,
                             [(kwT[:, kt * 128:(kt + 1) * 128], qcols)],
                             [(LTa[:, dd, :], RTa[:, 4 + g, :])],
                             None, masks, pvl, post)
                def emit_sel(jt=jt, g=g, acc=acc, sT=sT, qcols=qcols, gate_scaled=gate_scaled, full4=full4):
                    oset = oset_views(osets[0], 129)
                    started = [False, False]
                    for kt in range(jt + 1):
                        last = kt == jt
                        masks = [(0, 512, full4, 0, -1)] if last else []
                        pvl = []
                        for j in range(4):
                            bnk = j // 2
                            pvl.append((oset[j], 128 * j, 128 * j + 128, vs.v(vs.ap[:, kt, :]), not started[bnk], last))
                            started[bnk] = True
                        post = []
                        if last:
                            def fin_sel(oset=oset, acc=acc, jt=jt, g=g, gate_scaled=gate_scaled):
                                gate_scaled(oset, 129, 1, False)
                                ob = obrot.next()
                                k.copy(ob.all, acc.all, en="dve")
                                dsts = [oT.v(oT.ap[8 + 4 * g + j, :, jt * 128:(jt + 1) * 128]) for j in range(4)]
                                pipe.defer(2, lambda: fin_transposes(ob, dsts))
                            post.append(fin_sel)
                        att_tile(pipe, strot, prot, 128, 0,
                                 [(ksT[:, kt * 128:(kt + 1) * 128], qcols)],
                                 [(EKT[:, kt, :], sT.v(sT.ap.rearrange("p a b -> p (a b)")))],
                                 None, masks, pvl, post)

                if pending_sel[0] is not None:
                    pending_sel[0]()
                pending_sel[0] = emit_sel
            pipe.flush()
            pending_sel[0]()
            pending_sel[0] = None
            pipe.flush()
        k.barrier()
        k.take_sems([q4, ksT, kwT, vs, vw, ovf] + otrot.bufs)
        nst.close()
        caster.finish([(n_, l) for n_ in REST_JOBS] + ([("w_in", l + 1)] if l + 1 < NL else []))
        k.barrier()
        caster.release()
        st.close()


    def gen_tail(ysb, ssbank, rstd, src, t0, TB, gi, xrot, gnl, nxt=None):
        k.act(rstd.all, ssbank[:, 0:TB], AF.Sqrt, scale=1.0 / D, bias=EPS)
        k.recip(rstd.all, rstd.all)
        yield
        for f in range(KC):
            xb = xrot.next()
            k.dma(xb.all, src[f][:, t0:t0 + TB])
            k.tt(ysb[f].all, ysb[f].all, rstd.all, ALU.mult, en="dve")
            k.stt(xb.all, ysb[f].all, gnl[:, gi, f:f + 1], xb.all, ALU.mult, ALU.add)
            k.dma(xres[f][:, t0:t0 + TB], xb.all, pw=True)
            if nxt is not None:
                bank, sqr, r2 = nxt
                sq = sqr.next()
                k.tt(sq.all, xb.all, xb.all, ALU.mult)
                k.mm(bank[:, 0:TB], ones_b.all, sq.all, start=(f == 0), stop=(f == KC - 1), inc=True)
            yield
        if nxt is not None:
            bank, sqr, r2 = nxt
            k.act(r2.all, bank[:, 0:TB], AF.Sqrt, scale=1.0 / D, bias=EPS)
            k.recip(r2.all, r2.all)
            k.dma(rsd[0:1, t0:t0 + TB], r2[0:1, :], pw=True)
            yield

    def y_evac(ps, ysb_f, sqrot, ssbank, f):
        k.act(ysb_f.all, ps[:, :], AF.Copy)
        sq = sqrot.next()
        k.tt(sq.all, ysb_f.all, ysb_f.all, ALU.mult)
        k.mm(ssbank[:, :], ones_b.all, sq.all, start=(f == 0), stop=(f == KC - 1), inc=True)

    def phase3(l, src):
        TB = 512
        st = ExitStack()
        hrot = Rot([k.give_sem(k.sb(st, "hblk%d" % i, [128, KC, TB], BF16)) for i in range(2)])
        orot = Rot([k.give_sem(k.sb(st, "oblk%d" % i, [128, KC, TB], BF16)) for i in range(2)])
        merged = [k.sb(st, "mg%d" % f, [128, TB], BF16) for f in range(KC)]
        ysb = [k.sb(st, "ysb%d" % f, [128, TB], F32) for f in range(KC)]
        gsl = Rot([k.give_sem(k.sb(st, "gsl%d" % i, [128, KC, 512], BF16)) for i in range(3)])
        wsl = Rot([k.give_sem(k.sb(st, "wsl%d" % i, [128, 8, 512], BF16)) for i in range(2)])
        mt = [k.sb(st, "mt%d" % i, [128, TB], F32) for i in range(4)]
        sgrot = Rot([k.sb(st, "sg%d" % i, [128, TB], F32) for i in range(2)])
        sqrot = Rot([k.sb(st, "sq3_%d" % i, [128, TB], BF16) for i in range(2)])
        rstd = k.sb(st, "rstd3", [128, TB], F32)
        rstd2 = k.give_sem(k.sb(st, "rstd3b", [128, TB], F32))
        xrot = Rot([k.give_sem(k.sb(st, "xr3_%d" % i, [128, TB], F32)) for i in range(2)])
        wg = W16["w_gate"][l]
        wbr = [("wb_fox", 4, 0), ("wb_diff", 4, 4), ("wb_nsa", 8, 8)]
        wo = W16["w_out"][l]
        grot = Rot([PS[0], PS[1]])
        yrot = Rot([PS[2], PS[3]])
        orot_ps = Rot([PS[4], PS[5]])
        ssb = PS[6]
        NB3 = S // TB

        def load_blk(blk_):
            hb_ = hrot.next()
            ob_ = orot.next()
            k.dma(hb_.all, hT.v(hT.ap[:, :, blk_ * TB:(blk_ + 1) * TB].rearrange("c p s -> p c s")))
            k.dma(ob_.all, oT.v(oT.ap[:, :, blk_ * TB:(blk_ + 1) * TB].rearrange("c p s -> p c s")))
            return hb_, ob_

        nxt = load_blk(0)
        tail = None
        for blk in range(NB3):
            t0 = blk * TB
            hb, ob = nxt
            if blk + 1 < NB3:
                nxt = load_blk(blk + 1)
            bg = BG([tail] if tail is not None else [])
            for fg in range(4):
                for b in range(3):
                    wbuf, kcb, cb0 = wbr[b]
                    gs_ = gsl.next()
                    ws_ = wsl.next()
                    k.dma(gs_.all, wslab("w_gate", l, 0, KC, b * D + fg * 512, 512))
                    k.dma(ws_.v(ws_.ap[:, 0:kcb, :]), wslab(wbuf, l, 0, kcb, fg * 512, 512))
                    for fi in range(4):
                        f = fg * 4 + fi
                        G = grot.next()
                        Y = yrot.next()
                        for c in range(KC):
                            k.mm(G[:, :], gs_.v(gs_.ap[:, c, fi * 128:(fi + 1) * 128]), hb.v(hb.ap[:, c, :]), start=(c == 0), stop=(c == KC - 1))
                        for c in range(kcb):
                            k.mm(Y[:, :], ws_.v(ws_.ap[:, c, fi * 128:(fi + 1) * 128]), ob.v(ob.ap[:, cb0 + c, :]), start=(c == 0), stop=(c == kcb - 1))
                        sg = sgrot.next()
                        k.act(sg.all, G[:, :], AF.Sigmoid)
                        if b == 0:
                            k.tt(mt[fi].all, sg.all, Y[:, :], ALU.mult)
                        elif b == 1:
                            k.tt(sg.all, sg.all, Y[:, :], ALU.mult)
                            k.tt(mt[fi].all, mt[fi].all, sg.all, ALU.add, en="dve")
                        else:
                            k.tt(sg.all, sg.all, Y[:, :], ALU.mult)
                            k.tt(merged[f].all, mt[fi].all, sg.all, ALU.add, en="dve")
                        bg.step(1)
            bg.drain()
            for fg in range(4):
                gs_ = gsl.next()
                k.dma(gs_.all, wslab("w_out", l, 0, KC, fg * 512, 512))
                for fi in range(4):
                    f = fg * 4 + fi
                    Y = orot_ps.next()
                    for c in range(KC):
                        k.mm(Y[:, :], gs_.v(gs_.ap[:, c, fi * 128:(fi + 1) * 128]), merged[c].all, start=(c == 0), stop=(c == KC - 1))
                    y_evac(Y, ysb[f], sqrot, ssb, f)
            tail = gen_tail(ysb, ssb, rstd, src, t0, TB, 1, xrot, gn_l[l], nxt=(PS[4], sqrot, rstd2))
        run_gen(tail)
        k.barrier()
        k.take_sems(hrot.bufs + orot.bufs + gsl.bufs + wsl.bufs + xrot.bufs + [rstd2])
        st.close()

    def phase4(l):
        TB = 512
        NF = DFF // 128
        st = ExitStack()
        hbs = [[k.sb(st, "h4_%d_%d" % (i, c), [128, TB], BF16) for c in range(KC)] for i in range(2)]
        act_ = [k.sb(st, "a4_%d" % f, [128, TB], BF16) for f in range(NF)]
        ysb = [k.sb(st, "y4_%d" % f, [128, TB], F32) for f in range(KC)]
        slr = Rot([k.give_sem(k.sb(st, "sl4_%d" % i, [128, KC, 512], BF16)) for i in range(4)])
        sgrot = Rot([k.sb(st, "sg4_%d" % i, [128, TB], F32) for i in range(3)])
        sqrot = Rot([k.sb(st, "sq4_%d" % i, [128, TB], BF16) for i in range(2)])
        rstd = k.give_sem(k.sb(st, "rstd4", [128, TB], F32))
        xrot = Rot([k.give_sem(k.sb(st, "xr4_%d" % i, [128, TB], F32)) for i in range(3)])
        wu = W16["w_up"][l]
        wd = W16["w_dn"][l]
        grot = Rot([PS[0], PS[1]])
        urot = Rot([PS[2], PS[3]])
        ssb = PS[6]
        NB4 = S // TB
        run_gen(gen_norm(xres, 0, TB, hbs[0], xrot, sqrot, rstd, [PS[6]], rs_dram=rsd))
        tail = None
        for blk in range(NB4):
            t0 = blk * TB
            hb = hbs[blk % 2]
            gens = []
            if tail is not None:
                gens.append(tail)
            if blk + 1 < NB4:
                gens.append(gen_norm(xres, t0 + TB, TB, hbs[(blk + 1) % 2], xrot, sqrot, rstd, [PS[6]], rs_dram=rsd))
            bg = BG(gens)
            for s_ in range(NF // 4):
                gsl_ = slr.next()
                usl_ = slr.next()
                k.dma(gsl_.all, wslab("w_up", l, 0, KC, s_ * 512, 512))
                k.dma(usl_.all, wslab("w_up", l, 0, KC, DFF + s_ * 512, 512))
                for fi in range(4):
                    f = s_ * 4 + fi
                    G = grot.next()
                    U = urot.next()
                    for c in range(KC):
                        k.mm(G[:, :], gsl_.v(gsl_.ap[:, c, fi * 128:(fi + 1) * 128]), hb[c].all, start=(c == 0), stop=(c == KC - 1))
                    for c in range(KC):
                        k.mm(U[:, :], usl_.v(usl_.ap[:, c, fi * 128:(fi + 1) * 128]), hb[c].all, start=(c == 0), stop=(c == KC - 1))
                    sg = sgrot.next()
                    k.act(sg.all, G[:, :], AF.Silu)
                    k.tt(act_[f].all, sg.all, U[:, :], ALU.mult)
                    bg.step(2)
            bg.drain()
            for fg2 in range(D // 256):
                sls = []
                for half in range(2):
                    sl = slr.next()
                    k.dma(sl.v(sl.ap.rearrange("p a b -> p (a b)")[:, 0:22 * 256].rearrange("p (a b) -> p a b", b=256)),
                          wslab("w_dn", l, half * 22, 22, fg2 * 256, 256))
                    sls.append(sl.v(sl.ap.rearrange("p a b -> p (a b)")[:, 0:22 * 256].rearrange("p (a b) -> p a b", b=256)))
                Ys = [PS[4], PS[5]]
                for half in range(2):
                    for fi in range(2):
                        for c in range(22):
                            kc = half * 22 + c
                            k.mm(Ys[fi][:, :], sls[half][:, c, fi * 128:(fi + 1) * 128], act_[kc].all, start=(kc == 0), stop=(kc == NF - 1))
                for fi in range(2):
                    f = fg2 * 2 + fi
                    y_evac(Ys[fi], ysb[f], sqrot, ssb, f)
            tail = gen_tail(ysb, ssb, rstd, xres, t0, TB, 3, xrot, gn_l[l])
        run_gen(tail)
        k.barrier()
        k.take_sems(slr.bufs + xrot.bufs + [rstd])
        st.close()

    for l in range(NL):
        prep_small(l)
    st0 = ExitStack()
    caster.engs = ("dve", "act")
    caster.alloc(st0)
    add_cast_jobs_early(0)
    caster.finish([("w_in", 0)])
    k.barrier()
    caster.release()
    st0.close()
    caster.engs = ("dve",)
    for l in range(NL):
        phase1(l, xin if l == 0 else xres)
        if stop_after == ("p1", l):
            break
        add_cast_jobs_rest(l)
        if l + 1 < NL:
            add_cast_jobs_early(l + 1)
        phase2(l)
        if stop_after == ("p2", l):
            break
        phase3(l, xin if l == 0 else xres)
        if stop_after == ("p3", l):
            break
        phase4(l)

    k.barrier()
    gs.close()
    k.es.close()
    return nc


def host_inputs(inputs, b, S=4096, NL=4):
    f = np.float32
    NT = S // 128
    m = {}
    m["xT"] = np.ascontiguousarray(np.asarray(inputs["x"])[b, :S].T).astype(f)
    for n in ["w_in", "w_gate", "w_out", "w_branch_fox", "w_branch_diff", "w_branch_nsa", "w_ffn_up", "w_ffn_down",
              "nsa_cmp_w1", "nsa_cmp_w2"]:
        m[n] = np.ascontiguousarray(np.asarray(inputs[n])[:NL]).astype(f)
    g = np.asarray(inputs["norm_gains"])[:NL]
    m["gains"] = np.ascontiguousarray(g.reshape(NL, 4, KC, 128).transpose(0, 1, 3, 2)).astype(f)
    m["fbias"] = np.ascontiguousarray(np.asarray(inputs["fox_forget_bias"])[:NL].reshape(NL, 4, 1)).astype(f)
    m["dlam"] = np.ascontiguousarray(np.asarray(inputs["diff_lambda"])[:NL].reshape(NL, 256)).astype(f)
    m["subln"] = np.ascontiguousarray(np.asarray(inputs["diff_subln"])[:NL].reshape(NL, 128, 1)).astype(f)
    m["cposT"] = np.ascontiguousarray(np.asarray(inputs["nsa_cmp_pos"])[:NL].transpose(0, 1, 3, 2)).astype(f)
    m["c_ident"] = np.eye(128, dtype=f)
    ncmp = (S - 32) // 16 + 1
    c = np.arange(256)
    j = np.arange(64)
    ov = ((c[:, None] * 16 < j[None, :] * 64 + 64) & (c[:, None] * 16 + 31 >= j[None, :] * 64) & (c[:, None] < ncmp)).astype(f)
    m["c_ov"] = np.ascontiguousarray(ov.reshape(2, 128, 64).transpose(1, 0, 2))
    kt = np.arange(NT)
    p = np.arange(128)
    e = ((128 * kt[None, :, None] + p[None, None, :]) // 64 == j[:, None, None]).astype(f)
    m["c_ekt"] = np.ascontiguousarray(e)
    lt = np.zeros((3, 48, 128), f)
    lt[0] = p[None, :]
    lt[1] = 1.0
    lt[2] = (np.arange(48) - 31)[:, None]
    m["c_lt"] = lt
    rt = np.zeros((3, 8, 512), f)
    jb = np.arange(512) // 128
    for h in range(4):
        sl = 2.0 ** (-2.0 * (h + 1))
        rt[0, h] = sl
        rt[1, h] = -sl * (128 * jb + 64)
        rt[2, h] = 128 * sl
    for g in range(2):
        slc = 2.0 ** (-(4 * g + jb + 1.0))
        rt[0, 4 + g] = slc
        rt[1, 4 + g] = -64 * slc
        rt[2, 4 + g] = 128 * slc
        rt[0, 6 + g] = 16 * slc
        rt[1, 6 + g] = -33 * slc
        rt[2, 6 + g] = 128 * slc
    m["c_rt"] = rt
    oh = np.zeros((4, 4, 128), f)
    for h in range(4):
        oh[h, h] = 1.0
    m["c_oneh"] = oh
    sidx = np.arange(S)
    m["c_ka"] = np.stack([sidx % 128, np.ones(S), sidx // 128]).astype(f)
    m["c_qa"] = np.stack([np.ones(S), -(128 * (sidx // 128) + 64), 128 * np.ones(S)]).astype(f)
    ab = np.zeros((3, 2, 2, 4), f)
    for g in range(2):
        slc = 2.0 ** (-(4 * g + np.arange(4) + 1.0))
        ab[0, g, 0] = slc
        ab[1, g, 0] = -64 * slc
        ab[2, g, 0] = 128 * slc
        ab[1, g, 1] = -128 * slc
    m["c_selab"] = ab
    return m


def kernel(**inputs):
    S, NL = 4096, 4
    B = inputs["x"].shape[0]
    nc = build(S, NL)
    in_maps = [host_inputs(inputs, b, S, NL) for b in range(B)]
    res = run_bass_kernel_spmd(nc, in_maps, core_ids=list(range(B)))
    out = np.stack([np.ascontiguousarray(r["yT"].T) for r in res.results], axis=0)
    return out.astype(np.float32)
```

```python
import math
from contextlib import ExitStack
import numpy as np
import concourse.bass as bass
import concourse.mybir as mybir
from concourse.bass_utils import run_bass_kernel_spmd

F32 = mybir.dt.float32
BF16 = mybir.dt.bfloat16
I32 = mybir.dt.int32
AF = mybir.ActivationFunctionType
ALU = mybir.AluOpType

D = 2048
KC = 16
HD = 128
IN_COLS = 5660
DFF = 5632
EPS = 1e-6
NEG = -30000.0

C_FQ, C_FK, C_FV, C_FF = 0, 512, 1024, 1536
C_DQ, C_DK, C_DV = 1540, 2052, 2564
C_NQ, C_NKC, C_NVC, C_NKS, C_NVS, C_NKW, C_NVW, C_NG = 3076, 4100, 4356, 4612, 4868, 5124, 5380, 5636


class Sem:
    def __init__(self, h, name):
        self.h = h
        self.cnt = 0
        self.name = name


class Eng:
    def __init__(self, name, h, sem):
        self.name = name
        self.h = h
        self.sem = sem
        self.waited = {}


class Buf:
    def __init__(self, name, ap):
        self.name = name
        self.ap = ap
        self.w = {}
        self.r = {}
        self.dsem = None

    def __getitem__(self, idx):
        return V(self, self.ap[idx])

    def v(self, ap):
        return V(self, ap)

    @property
    def all(self):
        return V(self, self.ap)


class V:
    def __init__(self, buf, ap):
        self.buf = buf
        self.ap = ap

    def __getitem__(self, idx):
        return V(self.buf, self.ap[idx])


class KB:
    def __init__(self, nc):
        self.nc = nc
        self.es = ExitStack()
        self.eng = {}
        for n, h in [("pe", nc.tensor), ("act", nc.scalar), ("dve", nc.vector), ("pool", nc.gpsimd), ("sp", nc.sync)]:
            self.eng[n] = Eng(n, h, self.new_sem("e_" + n))
        self.dma_pool = [self.new_sem("d%d" % i) for i in range(90)]
        self.dma_free = list(self.dma_pool)
        self.all_sems = [e.sem for e in self.eng.values()] + self.dma_pool
        self.uid = 0
        self.rr = 0

    def new_sem(self, name):
        return Sem(self.es.enter_context(self.nc.semaphore(name)), name)

    def sbuf(self, st, name, shape, dt):
        self.uid += 1
        return st.enter_context(self.nc.sbuf_tensor("%s_%d" % (name, self.uid), list(shape), dt))

    def sb(self, st, name, shape, dt):
        t = self.sbuf(st, name, shape, dt)
        return Buf(name, t[:])

    def dram(self, name, shape, dt, kind="Internal"):
        t = self.nc.dram_tensor(name, list(shape), dt, kind=kind)
        return Buf(name, t.ap())

    def _deps(self, reads, writes, pwrites=()):
        deps = {}
        for b in reads:
            for s, v in b.w.items():
                if deps.get(s, 0) < v:
                    deps[s] = v
        for b in writes:
            for s, v in b.w.items():
                if deps.get(s, 0) < v:
                    deps[s] = v
            for s, v in b.r.items():
                if deps.get(s, 0) < v:
                    deps[s] = v
        for b in pwrites:
            for s, v in b.r.items():
                if deps.get(s, 0) < v:
                    deps[s] = v
        return deps

    def _wait(self, e, deps, skip_self=False):
        for s, v in deps.items():
            if skip_self and s is e.sem:
                continue
            if e.waited.get(s, 0) < v:
                e.h.wait_ge(s.h, v)
                e.waited[s] = v

    def op(self, en, fn, ins=(), outs=(), inc=True):
        e = self.eng[en]
        reads = [x.buf if isinstance(x, V) else x for x in ins if x is not None]
        writes = [x.buf if isinstance(x, V) else x for x in outs if x is not None]
        self._wait(e, self._deps(reads, writes), skip_self=(en == "pe"))
        inst = fn(e.h)
        if inc:
            inst.then_inc(e.sem.h, 1)
            e.sem.cnt += 1
            c = e.sem.cnt
        else:
            assert en == "pe"
            c = e.sem.cnt + 1
        for b in reads:
            b.r[e.sem] = c
        for b in writes:
            b.w = {e.sem: c}
            b.r = {}
        return inst

    def dma(self, out, in_, q=None, pw=False, sem_of=None):
        if q is None:
            q = "sp"
        e = self.eng[q]
        rb, wb = in_.buf, out.buf
        if pw:
            deps = self._deps([rb], [], [wb])
        else:
            deps = self._deps([rb], [wb])
        self._wait(e, deps)
        sb = sem_of if sem_of is not None else None
        if sb is None:
            sb = wb if wb.dsem is not None else rb
        assert sb.dsem is not None, (rb.name, wb.name)
        s = sb.dsem
        e.h.dma_start(out=out.ap, in_=in_.ap).then_inc(s.h, 16)
        s.cnt += 16
        rb.r[s] = s.cnt
        if pw:
            wb.w[s] = s.cnt
        else:
            wb.w = {s: s.cnt}
            wb.r = {}

    def give_sem(self, *bufs):
        for b in bufs:
            if b.dsem is None:
                b.dsem = self.dma_free.pop()
        return bufs[0] if len(bufs) == 1 else bufs

    def take_sems(self, bufs):
        for b in bufs:
            if b.dsem is not None:
                self.dma_free.append(b.dsem)
                b.dsem = None

    def barrier(self):
        for e in self.eng.values():
            for s in self.all_sems:
                if s is e.sem:
                    continue
                if s.cnt > 0 and e.waited.get(s, 0) < s.cnt:
                    e.h.wait_ge(s.h, s.cnt)
                    e.waited[s] = s.cnt

    def mm(self, out, lhsT, rhs, start=True, stop=True, extra_in=(), inc=None):
        if inc is None:
            inc = stop
        return self.op("pe", lambda h: h.matmul(out.ap, lhsT=lhsT.ap, rhs=rhs.ap, start=start, stop=stop),
                       ins=[lhsT, rhs] + list(extra_in), outs=[out], inc=inc)

    def transpose(self, out, in_, ident):
        return self.op("pe", lambda h: h.transpose(out.ap, in_.ap, ident.ap), ins=[in_, ident], outs=[out])

    def act(self, out, in_, func, bias=None, scale=None, accum=None, en="act"):
        kw = {}
        ins = [in_]
        if bias is not None:
            if isinstance(bias, V):
                kw["bias"] = bias.ap
                ins.append(bias)
            else:
                kw["bias"] = bias
        if scale is not None:
            if isinstance(scale, V):
                kw["scale"] = scale.ap
                ins.append(scale)
            else:
                kw["scale"] = scale
        outs = [out]
        if accum is not None:
            kw["accum_out"] = accum.ap
            outs.append(accum)
        return self.op("act", lambda h: h.activation(out=out.ap, in_=in_.ap, func=func, **kw), ins=ins, outs=outs)

    def tt(self, out, a, b, op, en="dve"):
        return self.op(en, lambda h: h.tensor_tensor(out=out.ap, in0=a.ap, in1=b.ap, op=op), ins=[a, b], outs=[out])

    def ts(self, out, a, s1, op0, s2=None, op1=None, en="dve"):
        ins = [a]
        kw = {}
        if isinstance(s1, V):
            ins.append(s1)
            s1 = s1.ap
        if isinstance(s2, V):
            ins.append(s2)
            s2 = s2.ap
        if op1 is not None:
            kw["op1"] = op1
        return self.op(en, lambda h: h.tensor_scalar(out=out.ap, in0=a.ap, scalar1=s1, scalar2=s2, op0=op0, **kw),
                       ins=ins, outs=[out])

    def stt(self, out, a, s, b, op0, op1):
        ins = [a, b]
        if isinstance(s, V):
            ins.append(s)
            s = s.ap
        return self.op("dve", lambda h: h.scalar_tensor_tensor(out=out.ap, in0=a.ap, scalar=s, in1=b.ap, op0=op0, op1=op1),
                       ins=ins, outs=[out])

    def copy(self, out, in_, en="dve"):
        if en == "act":
            return self.act(out, in_, AF.Copy)
        return self.op(en, lambda h: h.tensor_copy(out=out.ap, in_=in_.ap), ins=[in_], outs=[out])

    def memset(self, out, val, en="dve"):
        return self.op(en, lambda h: h.memset(out.ap, val), ins=[], outs=[out])

    def recip(self, out, in_):
        return self.op("dve", lambda h: h.reciprocal(out=out.ap, in_=in_.ap), ins=[in_], outs=[out])


class Rot:
    def __init__(self, bufs):
        self.bufs = bufs
        self.i = 0

    def next(self):
        b = self.bufs[self.i % len(self.bufs)]
        self.i += 1
        return b


def sub(ap_tensor_buf, offset_ap):
    return offset_ap


def build(S=4096, NL=4, dbg=(), stop_after=None):
    assert S % 1024 == 0
    NT = S // 128
    NQ = S // 512
    NCMP = (S - 32) // 16 + 1
    NSEL = S // 64
    nc = bass.Bass("TRN2", target_bir_lowering=False)
    k = KB(nc)

    def din(name, shape, dt=F32):
        return k.dram(name, shape, dt, kind="ExternalInput")

    def dscr(name, shape, dt):
        return k.dram(name, shape, dt, kind=("ExternalOutput" if name in dbg else "Internal"))

    x_in = din("xT", [D, S])
    w_in = din("w_in", [NL, D, IN_COLS])
    w_gate = din("w_gate", [NL, D, 3 * D])
    w_out = din("w_out", [NL, D, D])
    wb_fox = din("w_branch_fox", [NL, 512, D])
    wb_diff = din("w_branch_diff", [NL, 512, D])
    wb_nsa = din("w_branch_nsa", [NL, 1024, D])
    w_up = din("w_ffn_up", [NL, D, 2 * DFF])
    w_dn = din("w_ffn_down", [NL, DFF, D])
    cw1 = din("nsa_cmp_w1", [NL, 2, 4096, 256])
    cw2 = din("nsa_cmp_w2", [NL, 2, 256, 128])
    gains = din("gains", [NL, 4, 128, KC])
    fbias = din("fbias", [NL, 4, 1])
    dlam = din("dlam", [NL, 256])
    subln = din("subln", [NL, 128, 1])
    cpos = din("cposT", [NL, 2, 128, 32])
    c_ident = din("c_ident", [128, 128])
    c_ov = din("c_ov", [128, 2, 64])
    c_ekt = din("c_ekt", [64, NT, 128])
    c_lt = din("c_lt", [3, 48, 128])
    c_rt = din("c_rt", [3, 8, 512])
    c_oneh = din("c_oneh", [4, 4, 128])
    c_ka = din("c_ka", [3, S])
    c_qa = din("c_qa", [3, S])
    c_selab = din("c_selab", [3, 2, 2, 4])
    xres_t = nc.dram_tensor("yT", [D, S], F32, kind="ExternalOutput")
    xres = [Buf("xres%d" % c, xres_t.ap()[c * 128:(c + 1) * 128, :]) for c in range(KC)]
    xin = [Buf("xin%d" % c, x_in.ap[c * 128:(c + 1) * 128, :]) for c in range(KC)]

    hT = dscr("hT", [KC, 128, S], BF16)
    qkT = dscr("qkT", [32, 128, S], BF16)
    vtok = dscr("vtok", [NT, 128, 1536], BF16)
    oT = dscr("oT", [KC, 128, S], BF16)
    yscr = dscr("yscr", [KC, 128, S], F32)
    W16 = {}

    WLAY = {}

    def w16(name, K_, N_, lay=None):
        if lay is None:
            W16[name] = [dscr("%s16_%d" % (name, l), [128, K_ // 128, N_], BF16) for l in range(NL)]
        else:
            cws, ks = lay
            WLAY[name] = lay
            W16[name] = [dscr("%s16_%d" % (name, l), [N_ // cws, (K_ // 128) // ks, 128, ks, cws], BF16) for l in range(NL)]

    def wslab(name, l, kc0, kcn, n0, w):
        cws, ks = WLAY[name]
        buf = W16[name][l]
        assert n0 // cws == (n0 + w - 1) // cws and kc0 // ks == (kc0 + kcn - 1) // ks
        return buf.v(buf.ap[n0 // cws, kc0 // ks, :, kc0 % ks:kc0 % ks + kcn, n0 % cws:n0 % cws + w])

    def wstore_pieces(name, dst, k0, kg, n0, cw):
        if name not in WLAY:
            return [(dst.v(dst.ap[:, k0:k0 + kg, n0:n0 + cw]), (0, kg, 0, cw))]
        cws, ks = WLAY[name]
        out = []
        n = n0
        while n < n0 + cw:
            n_hi = min(n0 + cw, (n // cws + 1) * cws)
            kc = k0
            while kc < k0 + kg:
                kc_hi = min(k0 + kg, (kc // ks + 1) * ks)
                out.append((dst.v(dst.ap[n // cws, kc // ks, :, kc % ks:kc % ks + (kc_hi - kc), n % cws:n % cws + (n_hi - n)]),
                            (kc - k0, kc_hi - k0, n - n0, n_hi - n0)))
                kc = kc_hi
            n = n_hi
        return out

    w16("w_in", D, IN_COLS)
    w16("w_gate", D, 3 * D, (512, 16))
    w16("w_out", D, D, (512, 16))
    w16("wb_fox", 512, D, (512, 4))
    w16("wb_diff", 512, D, (512, 4))
    w16("wb_nsa", 1024, D, (512, 8))
    w16("w_up", D, 2 * DFF, (512, 16))
    w16("w_dn", DFF, D, (256, 22))
    w16("cw1k", 4096, 256)
    w16("cw1v", 4096, 256)
    w16("cw2k", 256, 128)
    w16("cw2v", 256, 128)

    gs = ExitStack()
    PS = [Buf("ps%d" % i, gs.enter_context(nc.psum_tensor("ps%d" % i, [128, 512], F32))[:]) for i in range(7)]
    PSB = Buf("psb", gs.enter_context(nc.psum_tensor("psb", [128, 1024], BF16))[:])
    ident_f = k.give_sem(k.sb(gs, "ident_f", [128, 128], F32))
    ident = k.sb(gs, "ident", [128, 128], BF16)
    ones_b = k.sb(gs, "ones_b", [128, 128], BF16)
    ones_f = k.sb(gs, "ones_f", [128, 128], F32)
    sel0_f = k.sb(gs, "sel0_f", [128, 128], F32)
    small_st = k.give_sem(k.sb(gs, "small_st", [128, 260], F32))
    k.dma(ident_f.all, c_ident.all)
    k.copy(ident.all, ident_f.all)
    k.memset(ones_b.all, 1.0)
    k.memset(ones_f.all, 1.0)
    k.memset(sel0_f.all, 0.0)
    k.memset(sel0_f[0:1, :], 1.0)

    KGC = 4
    CWC = 512

    class Caster:
        def __init__(self):
            self.jobs = []
            self.done = set()
            self.cur = None
            self.stg = None
            self.stb = None
            self.engs = ("dve",)
            self.alt = 0

        def alloc(self, st_):
            self.stg = Rot([k.give_sem(k.sb(st_, "cst_f%d" % i, [128, KGC, CWC], F32)) for i in range(2)])
            self.stb = Rot([k.give_sem(k.sb(st_, "cst_b%d" % i, [128, KGC, CWC], BF16)) for i in range(2)])

        def release(self):
            k.take_sems(self.stg.bufs + self.stb.bufs)
            self.stg = None
            self.stb = None

        def add(self, name, src_ap, K_, N_, dst, gain=None):
            self.jobs.append((name, self._gen(src_ap, K_, N_, dst, gain)))

        def _compute_store(self, pend, dst, gain):
            (n0, cw, k0, kg, f, b) = pend
            for j in range(kg):
                en = self.engs[self.alt % len(self.engs)]
                self.alt += 1
                o_ = b.v(b.ap[:, j, 0:cw])
                i_ = f.v(f.ap[:, j, 0:cw])
                g = None if gain is None else gain.buf.v(gain.ap[:, k0 + j:k0 + j + 1])
                if en == "act":
                    k.act(o_, i_, AF.Copy, scale=g)
                elif g is None:
                    k.copy(o_, i_, en=en)
                else:
                    k.ts(o_, i_, g, ALU.mult, en=en)
            for (dv, (ka, kb_, na, nb_)) in wstore_pieces(self.curname, dst, k0, kg, n0, cw):
                k.dma(dv, b.v(b.ap[:, ka:kb_, na:nb_]), q="pool", pw=True)

        def _gen(self, src_ap, K_, N_, dst, gain):
            kcs = K_ // 128
            src = Buf("wsrc", src_ap)
            srcv = src_ap.rearrange("(kc p) n -> p kc n", p=128)
            pend = None
            for n0 in range(0, N_, CWC):
                cw = min(CWC, N_ - n0)
                for k0 in range(0, kcs, KGC):
                    kg = min(KGC, kcs - k0)
                    f = self.stg.next()
                    b = self.stb.next()
                    k.dma(f.v(f.ap[:, 0:kg, 0:cw]), src.v(srcv[:, k0:k0 + kg, n0:n0 + cw]), q="sp")
                    if pend is not None:
                        self._compute_store(pend, dst, gain)
                        yield
                    pend = (n0, cw, k0, kg, f, b)
            self._compute_store(pend, dst, gain)
            yield

        def step(self):
            while True:
                if self.cur is None:
                    if not self.jobs:
                        return False
                    self.cur = self.jobs.pop(0)
                try:
                    self.curname = self.cur[0][0]
                    next(self.cur[1])
                    return True
                except StopIteration:
                    self.done.add(self.cur[0])
                    self.cur = None

        def finish(self, names):
            names = set(names)
            while not names <= self.done:
                if not self.step():
                    break
            assert names <= self.done, (names, self.done)

    caster = Caster()

    gn_l = [k.give_sem(k.sb(gs, "gn%d" % l, [128, 4, KC], F32)) for l in range(NL)]
    lamt_l = [k.sb(gs, "lamt%d" % l, [128, 4], F32) for l in range(NL)]
    subg_l = [k.sb(gs, "subg%d" % l, [128, 4], F32) for l in range(NL)]

    def prep_small(l):
        lam_init = 0.8 - 0.6 * math.exp(-0.3 * l)
        lamt = lamt_l[l]
        k.dma(gn_l[l].all, gains[l].buf.v(gains.ap[l].rearrange("i p c -> p i c")))
        k.dma(small_st[:, 0:256], dlam.v(bass.AP(dlam.ap.tensor, l * 256, [[0, 128], [1, 256]])))
        k.tt(small_st[:, 0:64], small_st[:, 0:64], small_st[:, 64:128], ALU.mult)
        k.tt(small_st[:, 128:192], small_st[:, 128:192], small_st[:, 192:256], ALU.mult)
        k.op("dve", lambda h: h.reduce_sum(out=small_st.ap[:, 256:257], in_=small_st.ap[:, 0:64], axis=mybir.AxisListType.X),
             ins=[small_st], outs=[small_st])
        k.op("dve", lambda h: h.reduce_sum(out=small_st.ap[:, 257:258], in_=small_st.ap[:, 128:192], axis=mybir.AxisListType.X),
             ins=[small_st], outs=[small_st])
        k.act(small_st[:, 256:258], small_st[:, 256:258], AF.Exp)
        k.tt(small_st[:, 258:259], small_st[:, 256:257], small_st[:, 257:258], ALU.subtract)
        k.ts(lamt[:, 0:1], small_st[:, 258:259], lam_init, ALU.add)
        k.ts(lamt[:, 1:2], lamt[:, 0:1], -1.0, ALU.mult)
        k.dma(small_st[:, 259:260], subln[l])
        k.ts(lamt[:, 2:3], small_st[:, 259:260], (1.0 - lam_init), ALU.mult)
        for j in range(4):
            k.copy(subg_l[l][:, j:j + 1], lamt[:, 2:3])

    def add_cast_jobs_early(l):
        caster.add(("w_in", l), w_in.ap[l], D, IN_COLS, W16["w_in"][l], gain=gn_l[l][:, 0, :])

    def add_cast_jobs_rest(l):
        caster.add(("cw1k", l), cw1.ap[l, 0], 4096, 256, W16["cw1k"][l])
        caster.add(("cw1v", l), cw1.ap[l, 1], 4096, 256, W16["cw1v"][l])
        caster.add(("cw2k", l), cw2.ap[l, 0], 256, 128, W16["cw2k"][l])
        caster.add(("cw2v", l), cw2.ap[l, 1], 256, 128, W16["cw2v"][l])
        caster.add(("w_gate", l), w_gate.ap[l], D, 3 * D, W16["w_gate"][l], gain=gn_l[l][:, 0, :])
        caster.add(("wb_fox", l), wb_fox.ap[l], 512, D, W16["wb_fox"][l])
        caster.add(("wb_diff", l), wb_diff.ap[l], 512, D, W16["wb_diff"][l], gain=subg_l[l].all)
        caster.add(("wb_nsa", l), wb_nsa.ap[l], 1024, D, W16["wb_nsa"][l])
        caster.add(("w_out", l), w_out.ap[l], D, D, W16["w_out"][l])
        caster.add(("w_up", l), w_up.ap[l], D, 2 * DFF, W16["w_up"][l], gain=gn_l[l][:, 2, :])
        caster.add(("w_dn", l), w_dn.ap[l], DFF, D, W16["w_dn"][l])

    CMP_JOBS = ["cw1k", "cw1v", "cw2k", "cw2v"]
    REST_JOBS = ["w_gate", "wb_fox", "wb_diff", "wb_nsa", "w_out", "w_up", "w_dn"]

    def gen_norm(src, t0, TB, hb, xrot, sqrot, rstd, psA, keep=None, rs_dram=None):
        nsub = TB // 512
        if rs_dram is not None:
            k.dma(rstd.all, rs_dram.v(bass.AP(rs_dram.ap.tensor, t0, [[0, 128], [1, TB]])))
            for c in range(KC):
                xb = xrot.next()
                k.dma(xb.all, src[c][:, t0:t0 + TB])
                k.tt(hb[c].all, xb.all, rstd.all, ALU.mult, en="dve")
                yield
            return
        for c in range(KC):
            xb = keep[c] if keep is not None else xrot.next()
            k.dma(xb.all, src[c][:, t0:t0 + TB])
            sq = sqrot.next()
            if c % 2 == 0:
                k.act(sq.all, xb.all, AF.Square)
            else:
                k.tt(sq.all, xb.all, xb.all, ALU.mult, en="dve")
            for n in range(nsub):
                k.mm(psA[n][:, :], ones_b.all, sq[:, n * 512:(n + 1) * 512], start=(c == 0), stop=(c == KC - 1), inc=True)
            yield
        for n in range(nsub):
            k.act(rstd[:, n * 512:(n + 1) * 512], psA[n][:, :], AF.Sqrt, scale=1.0 / D, bias=EPS)
        k.recip(rstd.all, rstd.all)
        for c in range(KC):
            if keep is not None:
                xb = keep[c]
            else:
                xb = xrot.next()
                k.dma(xb.all, src[c][:, t0:t0 + TB])
            k.tt(hb[c].all, xb.all, rstd.all, ALU.mult, en="dve")
            yield

    def run_gen(g):
        for _ in g:
            pass

    class BG:
        def __init__(self, gens):
            self.gens = list(gens)

        def step(self, n=1):
            for _ in range(n):
                while self.gens:
                    try:
                        next(self.gens[0])
                        break
                    except StopIteration:
                        self.gens.pop(0)

        def drain(self):
            while self.gens:
                self.step()

    FM_SLABS = [
        (C_FQ, 512, [0, 1, 2, 3]), (C_FK, 512, [4, 5, 6, 7]),
        (C_DQ, 512, [8, 9, 10, 11]), (C_DK, 512, [12, 13, 14, 15]),
        (C_NQ, 512, [16, 17, 18, 19]), (C_NQ + 512, 512, [20, 21, 22, 23]),
        (C_NKC, 512, [24, 25, 26, 27]), (C_NKS, 256, [28, 29]), (C_NKW, 256, [30, 31]),
    ]
    TM_SLABS = [(C_FV, 512, 0), (C_DV, 512, 512), (C_NVS, 256, 1024), (C_NVW, 256, 1280)]
    QSCALE = {}
    for t_ in [0, 1, 2, 3] + list(range(16, 24)):
        QSCALE[t_] = 128.0 ** -0.5
    for t_ in [8, 9, 10, 11]:
        QSCALE[t_] = 0.125

    rsd = dscr("rsd", [1, S], F32)
    lfd = dscr("lfd", [4, S], F32)
    lfcarry = k.sb(gs, "lfcarry", [4, 1], F32)
    nsag = k.sb(gs, "nsag", [128, NT, 24], F32)
    negfb = k.give_sem(k.sb(gs, "negfb", [4, 2], F32))

    def phase1(l, src):
        TB = 1024
        st = ExitStack()
        hbs = [[k.give_sem(k.sb(st, "h%d_%d" % (i, c), [128, TB], BF16)) for c in range(KC)] for i in range(2)]
        xrot = Rot([k.give_sem(k.sb(st, "xr%d" % i, [128, TB], F32)) for i in range(1)])
        xkeep = [k.give_sem(k.sb(st, "xk%d" % c, [128, TB], F32)) for c in range(KC)]
        sqrot = Rot([k.sb(st, "sq%d" % i, [128, TB], BF16) for i in range(2)])
        rstd = k.sb(st, "rstd", [128, TB], F32)
        slabs = Rot([k.give_sem(k.sb(st, "slab%d" % i, [128, KC, 512], BF16)) for i in range(2)])
        ostg = Rot([k.give_sem(k.sb(st, "ostg%d" % i, [128, TB], BF16)) for i in range(3)])
        vstg = Rot([k.give_sem(k.sb(st, "vstg%d" % i, [128, 512], BF16)) for i in range(3)])
        ffst = k.sb(st, "ffst", [4, 512], F32)
        lfblk = k.give_sem(k.sb(st, "lfblk", [4, TB], F32))
        w = W16["w_in"][l]
        k.dma(negfb[:, 0:1], fbias[l])
        k.ts(negfb[:, 1:2], negfb[:, 0:1], -1.0, ALU.mult)
        psrot = Rot(PS[2:6])
        ev = 0
        run_gen(gen_norm(src, 0, TB, hbs[0], xrot, sqrot, rstd, PS[0:2], keep=xkeep))
        for tb in range(S // TB):
            t0 = tb * TB
            hb = hbs[tb % 2]
            bg = BG([gen_norm(src, t0 + TB, TB, hbs[(tb + 1) % 2], xrot, sqrot, rstd, PS[0:2], keep=xkeep)] if tb + 1 < S // TB else [])
            for c in range(KC):
                k.dma(hT.v(hT.ap[c, :, t0:t0 + TB]), hb[c].all, pw=True)
            for (c0, wd, tiles) in FM_SLABS:
                sl = slabs.next()
                k.dma(sl.v(sl.ap[:, :, 0:wd]), w.v(w.ap[:, :, c0:c0 + wd]))
                for mi, tid in enumerate(tiles):
                    og = ostg.next()
                    for n in range(TB // 512):
                        ps = psrot.next()
                        for c in range(KC):
                            k.mm(ps[:, :], sl.v(sl.ap[:, c, mi * 128:(mi + 1) * 128]), hb[c][:, n * 512:(n + 1) * 512],
                                 start=(c == 0), stop=(c == KC - 1))
                        qs = QSCALE.get(tid, 1.0)
                        if ev % 2 == 0:
                            k.act(og[:, n * 512:(n + 1) * 512], ps[:, :], AF.Copy, scale=qs)
                        else:
                            k.ts(og[:, n * 512:(n + 1) * 512], ps[:, :], qs, ALU.mult)
                        ev += 1
                    k.dma(qkT.v(qkT.ap[tid, :, t0:t0 + TB]), og.all, pw=True)
                    bg.step(2)
            bg.drain()
            for (c0, wd, vc0) in TM_SLABS:
                sl = slabs.next()
                k.dma(sl.v(sl.ap[:, :, 0:wd]), w.v(w.ap[:, :, c0:c0 + wd]))
                for tt_ in range(TB // 128):
                    ps = psrot.next()
                    for c in range(KC):
                        k.mm(ps[:, 0:wd], hb[c][:, tt_ * 128:(tt_ + 1) * 128], sl.v(sl.ap[:, c, 0:wd]),
                             start=(c == 0), stop=(c == KC - 1))
                    vg = vstg.next()
                    if ev % 2 == 0:
                        k.act(vg[:, 0:wd], ps[:, 0:wd], AF.Copy)
                    else:
                        k.copy(vg[:, 0:wd], ps[:, 0:wd])
                    ev += 1
                    k.dma(vtok.v(vtok.ap[t0 // 128 + tt_, :, vc0:vc0 + wd]), vg[:, 0:wd], pw=True)
            sl = slabs.next()
            k.dma(sl.v(sl.ap[:, :, 0:4]), w.v(w.ap[:, :, C_FF:C_FF + 4]))
            k.dma(sl.v(sl.ap[:, :, 8:32]), w.v(w.ap[:, :, C_NG:C_NG + 24]))
            for n in range(TB // 512):
                ps = psrot.next()
                for c in range(KC):
                    k.mm(ps[0:4, :], sl.v(sl.ap[:, c, 0:4]), hb[c][:, n * 512:(n + 1) * 512], start=(c == 0), stop=(c == KC - 1))
                k.act(ffst[:, :], ps[0:4, :], AF.Exp, scale=-1.0, bias=negfb[:, 1:2])
                k.act(ffst[:, :], ffst[:, :], AF.Ln, bias=1.0)
                k.ts(ffst[:, :], ffst[:, :], -1.0, ALU.mult)
                a0 = n * 512
                if n > 0:
                    init = lfblk.ap[:, a0 - 1:a0]
                    init_ins = []
                elif tb > 0:
                    init = lfcarry.ap[:, 0:1]
                    init_ins = [lfcarry]
                else:
                    init = 0.0
                    init_ins = []
                k.op("dve", lambda h, a0=a0, init=init: h.tensor_tensor_scan(
                    out=lfblk.ap[:, a0:a0 + 512], data0=onesrow.ap[0:4, :], data1=ffst.ap[:, :],
                    initial=init, op0=ALU.mult, op1=ALU.add), ins=[ffst, onesrow, lfblk] + init_ins, outs=[lfblk])
            k.copy(lfcarry.all, lfblk[:, TB - 1:TB])
            k.dma(lfd[:, t0:t0 + TB], lfblk.all, pw=True)
            for tt_ in range(TB // 128):
                ps = psrot.next()
                for c in range(KC):
                    k.mm(ps[:, 0:24], hb[c][:, tt_ * 128:(tt_ + 1) * 128], sl.v(sl.ap[:, c, 8:32]), start=(c == 0), stop=(c == KC - 1))
                k.act(nsag[:, t0 // 128 + tt_, :], ps[:, 0:24], AF.Sigmoid)
        k.barrier()
        k.take_sems(hbs[0] + hbs[1] + xrot.bufs + xkeep + slabs.bufs + ostg.bufs + vstg.bufs + [lfblk])
        st.close()

    onesrow = k.sb(gs, "onesrow", [4, 512], F32)
    k.memset(onesrow.all, 1.0)

    class Pipe:
        def __init__(self, depth=2, bg=None, bg_every=4):
            self.q = []
            self.deferred = []
            self.depth = depth
            self.bg = bg
            self.bg_every = bg_every
            self.n = 0

        def push(self, t):
            t["qk"]()
            self.q.append(t)
            if len(self.q) > self.depth:
                self._finish(self.q.pop(0))

        def _finish(self, t):
            t["sm"]()
            t["pv"]()
            for fn in t.get("post", ()):
                fn()
            for d in self.deferred:
                d[0] -= 1
            ready = [d for d in self.deferred if d[0] <= 0]
            self.deferred = [d for d in self.deferred if d[0] > 0]
            for d in ready:
                d[1]()
            self.n += 1
            if self.bg is not None and self.n % self.bg_every == 0:
                self.bg()

        def defer(self, n, fn):
            self.deferred.append([n, fn])

        def flush(self):
            while self.q:
                self._finish(self.q.pop(0))
            for d in self.deferred:
                d[1]()
            self.deferred = []

    def att_tile(pipe, strot, prot, nk, col0, qk_list, aux_list, bias, masks, pv_list, post=()):
        ps = strot.next()
        pt = prot.next()
        mms = list(qk_list) + list(aux_list)

        def qk():
            for i, (lt, rh) in enumerate(mms):
                k.mm(ps[0:nk, col0:512], lt, rh, start=(i == 0), stop=(i == len(mms) - 1))

        def sm():
            k.act(pt[0:nk, col0:512], ps[0:nk, col0:512], AF.Exp, bias=bias)
            for (c_lo, c_hi, pattern, base, cm) in masks:
                k.op("pool", lambda h, c_lo=c_lo, c_hi=c_hi, pattern=pattern, base=base, cm=cm: h.affine_select(
                    out=pt.ap[0:nk, c_lo:c_hi], in_=pt.ap[0:nk, c_lo:c_hi], pattern=pattern, compare_op=ALU.is_ge,
                    fill=0.0, base=base, channel_multiplier=cm), ins=[pt], outs=[pt])

        def pv():
            for i_, (o, c_lo, c_hi, rh, start, stop) in enumerate(pv_list):
                k.mm(o, pt[0:nk, c_lo:c_hi], rh, start=start, stop=stop, inc=(i_ == len(pv_list) - 1))

        pipe.push({"qk": qk, "sm": sm, "pv": pv, "post": list(post)})

    def oset_views(banks, w):
        return [banks[j // 2].v(banks[j // 2].ap[:, (j % 2) * w:(j % 2) * w + w]) for j in range(4)]

    def phase2(l):
        st = ExitStack()
        caster.alloc(st)
        pipe = Pipe(depth=2, bg=caster.step, bg_every=8)
        strot = Rot(PS[0:3])
        prot = Rot([k.sb(st, "pt%d" % i, [128, 512], BF16) for i in range(3)])
        osets = [PS[3:5], PS[5:7]]
        denrot = Rot([k.sb(st, "den%d" % i, [128, 8], F32) for i in range(4)])
        obrot = Rot([k.sb(st, "ob%d" % i, [128, 4, 128], BF16) for i in range(2)])
        otrot = Rot([k.give_sem(k.sb(st, "ot%d" % i, [128, 512], BF16)) for i in range(2)])
        tfrot = Rot([k.sb(st, "tf%d" % i, [128, 4, 128], F32) for i in range(2)])
        psb_half = [0]

        LTa = k.sb(st, "LTa", [128, 48, 128], BF16)
        RTa = k.sb(st, "RTa", [128, 8, 512], BF16)
        EKT = k.sb(st, "EKT", [128, NT, 128], BF16)
        SAB = k.sb(st, "SAB", [128, 2, 2, 4], F32)
        hst = ExitStack()
        lf = k.give_sem(k.sb(hst, "lf", [4, S], F32))
        k.dma(lf.all, lfd.all)
        negcum = k.sb(hst, "negcum", [128, NT, 4], F32)
        Aall = k.sb(hst, "Aall", [128, S], BF16)
        k.memset(Aall.all, 0.0)
        CA = k.sb(hst, "CA", [128, S], BF16)
        CB = k.sb(hst, "CB", [128, S], BF16)
        ONEH = k.sb(hst, "ONEH", [128, 4, 128], BF16)
        cstk = ExitStack()
        cst = k.give_sem(k.sb(cstk, "cst_f", [128, max(S, 6144)], F32))
        for tbuf in (LTa, RTa, ONEH, EKT, CA, CB, SAB):
            k.memset(tbuf.all, 0.0)
        fl = "r a b -> r (a b)"
        k.dma(cst.v(cst.ap[0:3, 0:48 * 128]), c_lt.v(c_lt.ap.rearrange(fl)))
        k.copy(LTa.v(LTa.ap[0:3].rearrange(fl)), cst.v(cst.ap[0:3, 0:48 * 128]))
        k.dma(cst.v(cst.ap[0:3, 0:8 * 512]), c_rt.v(c_rt.ap.rearrange(fl)))
        k.copy(RTa.v(RTa.ap[0:3].rearrange(fl)), cst.v(cst.ap[0:3, 0:8 * 512]))
        k.dma(cst.v(cst.ap[0:4, 0:512]), c_oneh.v(c_oneh.ap.rearrange(fl)))
        k.copy(ONEH.v(ONEH.ap[0:4].rearrange(fl)), cst.v(cst.ap[0:4, 0:512]))
        k.dma(cst.v(cst.ap[0:64, 0:S]), c_ekt.v(c_ekt.ap.rearrange(fl)))
        k.dma(cst.v(cst.ap[64:67, 0:S]), c_ka.all)
        k.copy(EKT.v(EKT.ap[0:64].rearrange(fl)), cst.v(cst.ap[0:64, 0:S]))
        k.copy(EKT.v(EKT.ap[64:67].rearrange(fl)), cst.v(cst.ap[64:67, 0:S]))
        k.copy(CA[64:67, :], cst.v(cst.ap[64:67, 0:S]))
        k.dma(cst.v(cst.ap[0:3, 0:S]), c_ka.all)
        k.copy(CA[0:3, :], cst.v(cst.ap[0:3, 0:S]))
        k.dma(cst.v(cst.ap[0:3, 0:S]), c_qa.all)
        k.copy(CB[0:3, :], cst.v(cst.ap[0:3, 0:S]))
        k.dma(cst.v(cst.ap[64:67, 0:S]), c_qa.all)
        k.copy(CB[64:67, :], cst.v(cst.ap[64:67, 0:S]))
        k.dma(cst.v(cst.ap[64:67, 0:16]), c_selab.v(c_selab.ap.rearrange("r a b c -> r (a b c)")))
        k.copy(SAB.v(SAB.ap[64:67].rearrange("r a b c -> r (a b c)")), cst.v(cst.ap[64:67, 0:16]))
        k.barrier()
        k.take_sems([cst])
        cstk.close()

        for kt in range(NT):
            k.transpose(PS[0][:, kt * 4:(kt + 1) * 4], lf[0:4, kt * 128:(kt + 1) * 128], ident_f[0:4, 0:4])
        k.ts(negcum.v(negcum.ap.rearrange("p a b -> p (a b)")), PS[0][:, 0:NT * 4], -1.0, ALU.mult)
        lf3 = lf.ap.rearrange("p (a b) -> p a b", b=128)
        k.op("dve", lambda h: h.tensor_copy(out=Aall.ap[0:4].rearrange("p (a b) -> p a b", b=128),
                                            in_=bass.AP(lf3.tensor, lf3.offset, [list(lf3.ap[0]), list(lf3.ap[1]), [0, 128]])),
             ins=[lf], outs=[Aall])

        def fin_transposes(ob, dst_views):
            half = psb_half[0] % 2
            psb_half[0] += 1
            for j in range(4):
                k.transpose(PSB[:, half * 512 + j * 128: half * 512 + (j + 1) * 128], ob[:, j, :], ident.all)
            ot = otrot.next()
            k.copy(ot.all, PSB[:, half * 512:(half + 1) * 512])
            if len(dst_views) == 1:
                k.dma(dst_views[0], ot.all, pw=True)
            else:
                for j in range(4):
                    k.dma(dst_views[j], ot[:, j * 128:(j + 1) * 128], pw=True)

        qrot = Rot([k.give_sem(k.sb(hst, "qT%d" % i, [128, S], BF16)) for i in range(2)])
        krot = Rot([k.give_sem(k.sb(hst, "kT%d" % i, [128, S], BF16)) for i in range(2)])
        vrot = Rot([k.give_sem(k.sb(hst, "vh%d" % i, [128, NT, 129], BF16)) for i in range(2)])
        for vb in vrot.bufs:
            k.memset(vb.all, 1.0)

        def load_head(qtile, ktile, vcol):
            qb, kb, vb = qrot.next(), krot.next(), vrot.next()
            k.dma(qb.all, qkT[qtile])
            k.dma(kb.all, qkT[ktile])
            k.dma(vb.v(vb.ap[:, :, 0:128]), vtok.v(vtok.ap[:, :, vcol:vcol + 128].rearrange("t p c -> p t c")))
            return qb, kb, vb

        gcount = [0]
        for h in range(4):
            qb, kb, vb = load_head(h, 4 + h, h * 128)
            for qt in range(NQ):
                oset = oset_views(osets[gcount[0] % 2], 129)
                gcount[0] += 1
                started = [False, False]
                nkt = 4 * qt + 4
                for kt in range(nkt):
                    m = kt - 4 * qt
                    j0 = max(m, 0)
                    col0 = 128 * j0
                    masks = []
                    if m >= 0:
                        masks.append((128 * m, 128 * m + 128, [[1, 128]], 0, -1))
                    pvl = []
                    for j in range(j0, 4):
                        bnk = j // 2
                        pvl.append((oset[j], 128 * j, 128 * j + 128, vb.v(vb.ap[:, kt, :]), not started[bnk], kt == nkt - 1))
                        started[bnk] = True
                    post = []
                    if kt == nkt - 1:
                        def fin(h=h, qt=qt, oset=oset):
                            den = denrot.next()
                            ob = obrot.next()
                            for j in range(4):
                                k.ts(den[:, j:j + 1], oset[j][:, 128:129], 1e-30, ALU.max)
                                k.recip(den[:, j:j + 1], den[:, j:j + 1])
                                k.act(ob[:, j, :], oset[j][:, 0:128], AF.Copy, scale=den[:, j:j + 1])
                            pipe.defer(2, lambda: fin_transposes(ob, [oT.v(oT.ap[h, :, qt * 512:(qt + 1) * 512])]))
                        post.append(fin)
                    att_tile(pipe, strot, prot, 128, col0,
                             [(kb[:, kt * 128:(kt + 1) * 128], qb[:, qt * 512 + col0:(qt + 1) * 512])],
                             [(ONEH[:, h, :], Aall[:, qt * 512 + col0:(qt + 1) * 512])],
                             negcum[:, kt, h:h + 1], masks, pvl, post)
        pipe.flush()
        kz = [krot.bufs[0], krot.bufs[1]]
        qz = [qrot.bufs[0], qrot.bufs[1]]
        k.copy(kz[0][64:128, :], CA[64:128, :])
        k.copy(kz[1][0:64, :], CA[0:64, :])
        k.memset(qz[0][64:128, :], 0.0)
        k.memset(qz[1][0:64, :], 0.0)
        for h in range(4):
            slope_h = 2.0 ** (-2.0 * (h + 1))
            vb = vrot.next()
            k.dma(kz[0][0:64, :], qkT[12 + h][0:64, :])
            k.dma(kz[1][64:128, :], qkT[12 + h][64:128, :])
            k.dma(qz[0][0:64, :], qkT[8 + h][0:64, :])
            k.dma(qz[1][64:128, :], qkT[8 + h][64:128, :])
            k.ts(qz[0][64:67, :], CB[64:67, :], slope_h, ALU.mult)
            k.ts(qz[1][0:3, :], CB[0:3, :], slope_h, ALU.mult)
            k.dma(vb.v(vb.ap[:, :, 0:128]), vtok.v(vtok.ap[:, :, 512 + h * 128:512 + (h + 1) * 128].rearrange("t p c -> p t c")))
            for qt in range(NQ):
                nkt = 4 * qt + 4
                osm = [oset_views(osets[0], 129), oset_views(osets[1], 129)]
                for mp in range(2):
                    oset = osm[mp]
                    started = [False, False]
                    r0 = 64 * mp
                    for kt in range(nkt):
                        m = kt - 4 * qt
                        j0 = max(m, 0)
                        col0 = 128 * j0
                        masks = []
                        if m >= 0:
                            masks.append((128 * m, 128 * m + 128, [[1, 128]], 0, -1))
                        pvl = []
                        for j in range(j0, 4):
                            bnk = j // 2
                            pvl.append((oset[j], 128 * j, 128 * j + 128, vb.v(vb.ap[:, kt, :]), not started[bnk], kt == nkt - 1))
                            started[bnk] = True
                        post = []
                        if kt == nkt - 1 and mp == 1:
                            def fin(h=h, qt=qt, osm=osm):
                                den = denrot.next()
                                ob = obrot.next()
                                tf = tfrot.next()
                                for j in range(4):
                                    for mp_ in range(2):
                                        k.ts(den[:, 2 * j + mp_:2 * j + mp_ + 1], osm[mp_][j][:, 128:129], 1e-30, ALU.max)
                                    k.recip(den[:, 2 * j:2 * j + 2], den[:, 2 * j:2 * j + 2])
                                    k.tt(den[:, 2 * j + 1:2 * j + 2], den[:, 2 * j + 1:2 * j + 2], lamt_l[l][:, 1:2], ALU.mult)
                                    k.act(tf[:, j, :], osm[0][j][:, 0:128], AF.Copy, scale=den[:, 2 * j:2 * j + 1])
                                    k.stt(tf[:, j, :], osm[1][j][:, 0:128], den[:, 2 * j + 1:2 * j + 2], tf[:, j, :], ALU.mult, ALU.add)
                                den2 = denrot.next()
                                sq = tfrot.next()
                                for j in range(4):
                                    k.act(sq[:, j, :], tf[:, j, :], AF.Square, accum=den2[:, j:j + 1])
                                k.act(den2[:, 0:4], den2[:, 0:4], AF.Sqrt, scale=1.0 / 128.0, bias=EPS)
                                k.recip(den2[:, 0:4], den2[:, 0:4])
                                for j in range(4):
                                    k.ts(ob[:, j, :], tf[:, j, :], den2[:, j:j + 1], ALU.mult)
                                pipe.defer(2, lambda: fin_transposes(ob, [oT.v(oT.ap[4 + h, :, qt * 512:(qt + 1) * 512])]))
                            post.append(fin)
                        att_tile(pipe, strot, prot, 128, col0,
                                 [(kz[mp][:, kt * 128:(kt + 1) * 128], qz[mp][:, qt * 512 + col0:(qt + 1) * 512])],
                                 [],
                                 None, masks, pvl, post)
        pipe.flush()
        k.barrier()
        k.take_sems(qrot.bufs + krot.bufs + vrot.bufs + [lf])
        hst.close()

        nst = ExitStack()
        q4 = k.give_sem(k.sb(nst, "q4", [128, 4, S], BF16))
        ksT = k.give_sem(k.sb(nst, "ksT", [128, S], BF16))
        kwT = k.give_sem(k.sb(nst, "kwT", [128, S], BF16))
        vs = k.give_sem(k.sb(nst, "vs", [128, NT, 129], BF16))
        vw = k.give_sem(k.sb(nst, "vw", [128, NT, 129], BF16))
        kcT2 = k.sb(nst, "kcT2", [128, 2, 256], BF16)
        vcm2 = k.sb(nst, "vcm2", [128, 2, 2, 193], BF16)
        ovf = k.give_sem(k.sb(nst, "ovf", [128, 2, 64], F32))
        accrot = Rot([k.sb(nst, "acc%d" % i, [128, 4, 128], F32) for i in range(2)])
        imp = k.sb(nst, "imp", [128, 64], F32)
        imp2 = k.sb(nst, "imp2", [128, 64], F32)
        mx = k.sb(nst, "mx", [128, 16], F32)
        selb = k.sb(nst, "selb", [128, 64], BF16)
        selT4 = Rot([k.sb(nst, "selT4_%d" % i, [128, 4, 128], BF16) for i in range(2)])
        gsc = Rot([k.sb(nst, "gsc%d" % i, [128, 4], F32) for i in range(4)])
        for sb_ in selT4.bufs:
            k.memset(sb_.all, 0.0)
        k.memset(vs.all, 1.0)
        k.memset(vw.all, 1.0)
        k.memset(vcm2.all, 1.0)
        k.dma(ovf.all, c_ov.all)
        caster.finish([(n_, l) for n_ in CMP_JOBS])
        cstk2 = ExitStack()
        ncT = k.give_sem(k.sb(cstk2, "ncT", [128, S], BF16))
        Xc = k.sb(cstk2, "Xc", [128, 32, 256], BF16)
        w1s = k.give_sem(k.sb(cstk2, "w1s", [128, 32, 256], BF16))
        w2s = k.give_sem(k.sb(cstk2, "w2s", [128, 2, 128], BF16))
        hcT = k.sb(cstk2, "hcT", [128, 2, 256], BF16)
        posT = k.give_sem(k.sb(cstk2, "posT", [128, 2, 32], F32))
        k.memset(Xc.all, 0.0)
        k.dma(posT.all, cpos.v(cpos.ap[l].rearrange("a d b -> d a b")))

        def compress(g, kv):
            k.dma(ncT.all, qkT[24 + 2 * kv + g])
            k.dma(w1s.all, W16["cw1k" if kv == 0 else "cw1v"][l].all)
            k.dma(w2s.all, W16["cw2k" if kv == 0 else "cw2v"][l].all)
            nc3 = ncT.ap.rearrange("p (c s) -> p c s", s=16)
            for li in range(32):
                a_, b_ = li // 16, li % 16
                src = nc3[:, a_:a_ + NCMP, b_]
                k.ts(Xc.v(Xc.ap[:, li, 0:NCMP]), ncT.v(src), posT[:, kv, li:li + 1], ALU.add, en="dve")
            for m in range(2):
                ps = PS[m]
                for li in range(32):
                    k.mm(ps[:, 0:256], w1s.v(w1s.ap[:, li, m * 128:(m + 1) * 128]), Xc.v(Xc.ap[:, li, :]), start=(li == 0), stop=(li == 31))
                k.act(hcT[:, m, :], ps[:, 0:256], AF.Silu)
            if kv == 0:
                for m in range(2):
                    k.mm(PS[2][:, 0:256], w2s.v(w2s.ap[:, m, :]), hcT[:, m, :], start=(m == 0), stop=(m == 1))
                k.copy(kcT2[:, g, :], PS[2][:, 0:256])
            else:
                for ct in range(2):
                    for m in range(2):
                        k.mm(PS[2][:, ct * 128:(ct + 1) * 128], hcT[:, m, ct * 128:(ct + 1) * 128], w2s.v(w2s.ap[:, m, :]),
                             start=(m == 0), stop=(m == 1))
                    k.copy(vcm2[:, g, ct, 0:128], PS[2][:, ct * 128:(ct + 1) * 128])

        for g in range(2):
            compress(g, 0)
            compress(g, 1)
            k.copy(vcm2[:, g, :, 129:193], ovf.all)
        k.barrier()
        k.take_sems([ncT, w1s, w2s, posT])
        cstk2.close()

        for g in range(2):
            k.dma(q4.all, qkT.v(qkT.ap[16 + 4 * g:20 + 4 * g].rearrange("j p s -> p j s")))
            k.dma(ksT.all, qkT[28 + g])
            k.dma(kwT.all, qkT[30 + g])
            k.dma(vs.v(vs.ap[:, :, 0:128]), vtok.v(vtok.ap[:, :, 1024 + g * 128:1024 + (g + 1) * 128].rearrange("t p c -> p t c")))
            k.dma(vw.v(vw.ap[:, :, 0:128]), vtok.v(vtok.ap[:, :, 1280 + g * 128:1280 + (g + 1) * 128].rearrange("t p c -> p t c")))
            pending_sel = [None]
            for jt in range(NT):
                acc = accrot.next()
                sT = selT4.next()
                def bc(ap_):
                    return bass.AP(ap_.tensor, ap_.offset, [list(ap_.ap[0]), list(ap_.ap[1]), [0, 128]])
                k.stt(sT.v(sT.ap[64:67]), SAB.v(bc(SAB.ap[64:67, g, 1, :])), float(jt), SAB.v(bc(SAB.ap[64:67, g, 0, :])), ALU.mult, ALU.add)
                qcols = q4.v(q4.ap[:, :, jt * 128:(jt + 1) * 128])
                full4 = [[0, 4], [1, 128]]

                def gate_scaled(oset, w, br, first, jt=jt, g=g, acc=acc):
                    den = denrot.next()
                    gs_ = gsc.next()
                    for j in range(4):
                        k.ts(den[:, j:j + 1], oset[j][:, 128:129], 1e-30, ALU.max)
                    k.recip(den[:, 0:4], den[:, 0:4])
                    for j in range(4):
                        col = (4 * g + j) * 3 + br
                        k.tt(gs_[:, j:j + 1], den[:, j:j + 1], nsag[:, jt, col:col + 1], ALU.mult)
                        if first:
                            k.act(acc[:, j, :], oset[j][:, 0:128], AF.Copy, scale=gs_[:, j:j + 1])
                        else:
                            k.stt(acc[:, j, :], oset[j][:, 0:128], gs_[:, j:j + 1], acc[:, j, :], ALU.mult, ALU.add)
                    return den

                oset = oset_views(osets[0], 193)
                started = [False, False]
                cts = [(0, 128)]
                if jt >= 16:
                    cts.append((1, 128))
                for ci, (ct, nk) in enumerate(cts):
                    last = ci == len(cts) - 1
                    pvl = []
                    for j in range(4):
                        bnk = j // 2
                        pvl.append((oset[j], 128 * j, 128 * j + 128, vcm2.v(vcm2.ap[:, g, ct, :]), not started[bnk], last))
                        started[bnk] = True
                    post = []
                    if last:
                        def fin_cmp(oset=oset, jt=jt, sT=sT, gate_scaled=gate_scaled):
                            den = gate_scaled(oset, 193, 0, True)
                            for j in range(4):
                                if j == 0:
                                    k.ts(imp.all, oset[j][:, 129:193], den[:, 0:1], ALU.mult)
                                else:
                                    k.stt(imp.all, oset[j][:, 129:193], den[:, j:j + 1], imp.all, ALU.mult, ALU.add)
                            for hf in range(2):
                                cur = 2 * jt + hf
                                r0, r1 = 64 * hf, 64 * hf + 64
                                if cur + 1 < 64:
                                    k.memset(imp[r0:r1, cur + 1:64], -1.0)
                                k.memset(imp[r0:r1, 0:1], 1e4)
                                k.memset(imp[r0:r1, max(cur - 1, 0):cur + 1], 1e4)
                            k.op("dve", lambda h: h.max(out=mx.ap[:, 0:8], in_=imp.ap), ins=[imp], outs=[mx])
                            k.op("dve", lambda h: h.match_replace(out=imp2.ap, in_to_replace=mx.ap[:, 0:8], in_values=imp.ap, imm_value=-1e30),
                                 ins=[imp, mx], outs=[imp2])
                            k.op("dve", lambda h: h.max(out=mx.ap[:, 8:16], in_=imp2.ap), ins=[imp2, mx], outs=[mx])
                            k.ts(imp2.all, imp.all, mx[:, 15:16], ALU.is_ge)
                            k.ts(selb.all, imp2.all, 1.0, ALU.subtract, 30000.0, ALU.mult)

                            def tr():
                                half = psb_half[0] % 2
                                psb_half[0] += 1
                                k.transpose(PSB[0:64, half * 512:half * 512 + 128], selb.all, ident.all)
                                for j in range(4):
                                    k.copy(sT[0:64, j, :], PSB[0:64, half * 512:half * 512 + 128])
                            pipe.defer(1, tr)
                        post.append(fin_cmp)
                    dd = 16 * ct - jt + 31
                    att_tile(pipe, strot, prot, nk, 0,
                             [(kcT2[:, g, ct * 128:(ct + 1) * 128], qcols)],
                             [(LTa[:, dd, :], RTa[:, 6 + g, :])],
                             None, [(0, 512, full4, 128 * jt - 31 - 2048 * ct, -16)], pvl, post)
                oset = oset_views(osets[1], 129)
                started = [False, False]
                kts = list(range(max(0, jt - 4), jt + 1))
                for kt in kts:
                    last = kt == jt
                    masks = []
                    if kt == jt:
                        masks.append((0, 512, full4, 0, -1))
                    if kt == jt - 4:
                        masks.append((0, 512, [[0, 4], [-1, 128]], -1, 1))
                    pvl = []
                    for j in range(4):
                        bnk = j // 2
                        pvl.append((oset[j], 128 * j, 128 * j + 128, vw.v(vw.ap[:, kt, :]), not started[bnk], last))
                        started[bnk] = True
                    post = []
                    if last:
                        def fin_win(oset=oset, gate_scaled=gate_scaled):
                            gate_scaled(oset, 129, 2, False)
                        post.append(fin_win)
                    dd = kt - jt + 31
                    att_tile(pipe, strot, prot, 128, 0,
                             [(kwT[:, kt * 128:(kt + 1) * 128], qcols)],
                             [(LTa[:, dd, :], RTa[:, 4 + g, :])],
                             None, masks, pvl, post)
                def emit_sel(jt=jt, g=g, acc=acc, sT=sT, qcols=qcols, gate_scaled=gate_scaled, full4=full4):
                    oset = oset_views(osets[0], 129)
                    started = [False, False]
                    for kt in range(jt + 1):
                        last = kt == jt
                        masks = [(0, 512, full4, 0, -1)] if last else []
                        pvl = []
                        for j in range(4):
                            bnk = j // 2
                            pvl.append((oset[j], 128 * j, 128 * j + 128, vs.v(vs.ap[:, kt, :]), not started[bnk], last))
                            started[bnk] = True
                        post = []
                        if last:
                            def fin_sel(oset=oset, acc=acc, jt=jt, g=g, gate_scaled=gate_scaled):
                                gate_scaled(oset, 129, 1, False)
                                ob = obrot.next()
                                k.copy(ob.all, acc.all, en="dve")
                                dsts = [oT.v(oT.ap[8 + 4 * g + j, :, jt * 128:(jt + 1) * 128]) for j in range(4)]
                                pipe.defer(2, lambda: fin_transposes(ob, dsts))
                            post.append(fin_sel)
                        att_tile(pipe, strot, prot, 128, 0,
                                 [(ksT[:, kt * 128:(kt + 1) * 128], qcols)],
                                 [(EKT[:, kt, :], sT.v(sT.ap.rearrange("p a b -> p (a b)")))],
                                 None, masks, pvl, post)

                if pending_sel[0] is not None:
                    pending_sel[0]()
                pending_sel[0] = emit_sel
            pipe.flush()
            pending_sel[0]()
            pending_sel[0] = None
            pipe.flush()
        k.barrier()
        k.take_sems([q4, ksT, kwT, vs, vw, ovf] + otrot.bufs)
        nst.close()
        caster.finish([(n_, l) for n_ in REST_JOBS] + ([("w_in", l + 1)] if l + 1 < NL else []))
        k.barrier()
        caster.release()
        st.close()


    def gen_tail(ysb, ssbank, rstd, src, t0, TB, gi, xrot, gnl, nxt=None):
        k.act(rstd.all, ssbank[:, 0:TB], AF.Sqrt, scale=1.0 / D, bias=EPS)
        k.recip(rstd.all, rstd.all)
        yield
        for f in range(KC):
            xb = xrot.next()
            k.dma(xb.all, src[f][:, t0:t0 + TB])
            k.tt(ysb[f].all, ysb[f].all, rstd.all, ALU.mult, en="dve")
            k.stt(xb.all, ysb[f].all, gnl[:, gi, f:f + 1], xb.all, ALU.mult, ALU.add)
            k.dma(xres[f][:, t0:t0 + TB], xb.all, pw=True)
            if nxt is not None:
                bank, sqr, r2 = nxt
                if f > 0:
                    k.mm(bank[:, 0:TB], ones_b.all, prev_sq.all, start=(f == 1), stop=False, inc=True)
                sq = sqr.next()
                k.tt(sq.all, xb.all, xb.all, ALU.mult)
                prev_sq = sq
            yield
        if nxt is not None:
            bank, sqr, r2 = nxt
            k.mm(bank[:, 0:TB], ones_b.all, prev_sq.all, start=False, stop=True, inc=True)
            k.act(r2.all, bank[:, 0:TB], AF.Sqrt, scale=1.0 / D, bias=EPS)
            k.recip(r2.all, r2.all)
            k.dma(rsd[0:1, t0:t0 + TB], r2[0:1, :], pw=True)
            yield

    def y_evac(ps, ysb_f, sqrot, ssbank, f):
        k.act(ysb_f.all, ps[:, :], AF.Copy)
        sq = sqrot.next()
        k.tt(sq.all, ysb_f.all, ysb_f.all, ALU.mult)
        k.mm(ssbank[:, :], ones_b.all, sq.all, start=(f == 0), stop=(f == KC - 1), inc=True)

    def phase3(l, src):
        TB = 512
        st = ExitStack()
        hrot = Rot([k.give_sem(k.sb(st, "hblk%d" % i, [128, KC, TB], BF16)) for i in range(2)])
        orot = Rot([k.give_sem(k.sb(st, "oblk%d" % i, [128, KC, TB], BF16)) for i in range(2)])
        merged = [k.sb(st, "mg%d" % f, [128, TB], BF16) for f in range(KC)]
        ysb = [k.sb(st, "ysb%d" % f, [128, TB], F32) for f in range(KC)]
        gsl = Rot([k.give_sem(k.sb(st, "gsl%d" % i, [128, KC, 512], BF16)) for i in range(3)])
        wsl = Rot([k.give_sem(k.sb(st, "wsl%d" % i, [128, 8, 512], BF16)) for i in range(2)])
        mt = [k.sb(st, "mt%d" % i, [128, TB], F32) for i in range(4)]
        sgrot = Rot([k.sb(st, "sg%d" % i, [128, TB], F32) for i in range(2)])
        sqrot = Rot([k.sb(st, "sq3_%d" % i, [128, TB], BF16) for i in range(2)])
        rstd = k.sb(st, "rstd3", [128, TB], F32)
        rstd2 = k.give_sem(k.sb(st, "rstd3b", [128, TB], F32))
        xrot = Rot([k.give_sem(k.sb(st, "xr3_%d" % i, [128, TB], F32)) for i in range(2)])
        wg = W16["w_gate"][l]
        wbr = [("wb_fox", 4, 0), ("wb_diff", 4, 4), ("wb_nsa", 8, 8)]
        wo = W16["w_out"][l]
        grot = Rot([PS[0], PS[1]])
        yrot = Rot([PS[2], PS[3]])
        orot_ps = Rot([PS[4], PS[5]])
        ssb = PS[6]
        NB3 = S // TB

        def load_blk(blk_):
            hb_ = hrot.next()
            ob_ = orot.next()
            k.dma(hb_.all, hT.v(hT.ap[:, :, blk_ * TB:(blk_ + 1) * TB].rearrange("c p s -> p c s")))
            k.dma(ob_.all, oT.v(oT.ap[:, :, blk_ * TB:(blk_ + 1) * TB].rearrange("c p s -> p c s")))
            return hb_, ob_

        nxt = load_blk(0)
        tail = None
        for blk in range(NB3):
            t0 = blk * TB
            hb, ob = nxt
            if blk + 1 < NB3:
                nxt = load_blk(blk + 1)
            bg = BG([tail] if tail is not None else [])
            for fg in range(4):
                for b in range(3):
                    wbuf, kcb, cb0 = wbr[b]
                    gs_ = gsl.next()
                    ws_ = wsl.next()
                    k.dma(gs_.all, wslab("w_gate", l, 0, KC, b * D + fg * 512, 512))
                    k.dma(ws_.v(ws_.ap[:, 0:kcb, :]), wslab(wbuf, l, 0, kcb, fg * 512, 512))
                    for fi in range(4):
                        f = fg * 4 + fi
                        G = grot.next()
                        Y = yrot.next()
                        for c in range(KC):
                            k.mm(G[:, :], gs_.v(gs_.ap[:, c, fi * 128:(fi + 1) * 128]), hb.v(hb.ap[:, c, :]), start=(c == 0), stop=(c == KC - 1))
                        for c in range(kcb):
                            k.mm(Y[:, :], ws_.v(ws_.ap[:, c, fi * 128:(fi + 1) * 128]), ob.v(ob.ap[:, cb0 + c, :]), start=(c == 0), stop=(c == kcb - 1))
                        sg = sgrot.next()
                        k.act(sg.all, G[:, :], AF.Sigmoid)
                        if b == 0:
                            k.tt(mt[fi].all, sg.all, Y[:, :], ALU.mult)
                        elif b == 1:
                            k.tt(sg.all, sg.all, Y[:, :], ALU.mult)
                            k.tt(mt[fi].all, mt[fi].all, sg.all, ALU.add, en="dve")
                        else:
                            k.tt(sg.all, sg.all, Y[:, :], ALU.mult)
                            k.tt(merged[f].all, mt[fi].all, sg.all, ALU.add, en="dve")
                        bg.step(1)
            bg.drain()
            for fg in range(4):
                gs_ = gsl.next()
                k.dma(gs_.all, wslab("w_out", l, 0, KC, fg * 512, 512))
                for fi in range(4):
                    f = fg * 4 + fi
                    Y = orot_ps.next()
                    for c in range(KC):
                        k.mm(Y[:, :], gs_.v(gs_.ap[:, c, fi * 128:(fi + 1) * 128]), merged[c].all, start=(c == 0), stop=(c == KC - 1))
                    y_evac(Y, ysb[f], sqrot, ssb, f)
            tail = gen_tail(ysb, ssb, rstd, src, t0, TB, 1, xrot, gn_l[l], nxt=(PS[4], sqrot, rstd2))
        run_gen(tail)
        k.barrier()
        k.take_sems(hrot.bufs + orot.bufs + gsl.bufs + wsl.bufs + xrot.bufs + [rstd2])
        st.close()

    def phase4(l):
        TB = 512
        NF = DFF // 128
        st = ExitStack()
        hbs = [[k.sb(st, "h4_%d_%d" % (i, c), [128, TB], BF16) for c in range(KC)] for i in range(2)]
        act_ = [k.sb(st, "a4_%d" % f, [128, TB], BF16) for f in range(NF)]
        ysb = [k.sb(st, "y4_%d" % f, [128, TB], F32) for f in range(KC)]
        slr = Rot([k.give_sem(k.sb(st, "sl4_%d" % i, [128, KC, 512], BF16)) for i in range(4)])
        sgrot = Rot([k.sb(st, "sg4_%d" % i, [128, TB], F32) for i in range(3)])
        sqrot = Rot([k.sb(st, "sq4_%d" % i, [128, TB], BF16) for i in range(2)])
        rstd = k.give_sem(k.sb(st, "rstd4", [128, TB], F32))
        xrot = Rot([k.give_sem(k.sb(st, "xr4_%d" % i, [128, TB], F32)) for i in range(3)])
        wu = W16["w_up"][l]
        wd = W16["w_dn"][l]
        grot = Rot([PS[0], PS[1]])
        urot = Rot([PS[2], PS[3]])
        ssb = PS[6]
        NB4 = S // TB
        run_gen(gen_norm(xres, 0, TB, hbs[0], xrot, sqrot, rstd, [PS[6]], rs_dram=rsd))
        tail = None
        for blk in range(NB4):
            t0 = blk * TB
            hb = hbs[blk % 2]
            gens = []
            if tail is not None:
                gens.append(tail)
            if blk + 1 < NB4:
                gens.append(gen_norm(xres, t0 + TB, TB, hbs[(blk + 1) % 2], xrot, sqrot, rstd, [PS[6]], rs_dram=rsd))
            bg = BG(gens)
            for s_ in range(NF // 4):
                gsl_ = slr.next()
                usl_ = slr.next()
                k.dma(gsl_.all, wslab("w_up", l, 0, KC, s_ * 512, 512))
                k.dma(usl_.all, wslab("w_up", l, 0, KC, DFF + s_ * 512, 512))
                for fi in range(4):
                    f = s_ * 4 + fi
                    G = grot.next()
                    U = urot.next()
                    for c in range(KC):
                        k.mm(G[:, :], gsl_.v(gsl_.ap[:, c, fi * 128:(fi + 1) * 128]), hb[c].all, start=(c == 0), stop=(c == KC - 1))
                    for c in range(KC):
                        k.mm(U[:, :], usl_.v(usl_.ap[:, c, fi * 128:(fi + 1) * 128]), hb[c].all, start=(c == 0), stop=(c == KC - 1))
                    sg = sgrot.next()
                    k.act(sg.all, G[:, :], AF.Silu)
                    k.tt(act_[f].all, sg.all, U[:, :], ALU.mult)
                    bg.step(2)
            bg.drain()
            for fg2 in range(D // 256):
                sls = []
                for half in range(2):
                    sl = slr.next()
                    k.dma(sl.v(sl.ap.rearrange("p a b -> p (a b)")[:, 0:22 * 256].rearrange("p (a b) -> p a b", b=256)),
                          wslab("w_dn", l, half * 22, 22, fg2 * 256, 256))
                    sls.append(sl.v(sl.ap.rearrange("p a b -> p (a b)")[:, 0:22 * 256].rearrange("p (a b) -> p a b", b=256)))
                Ys = [PS[4], PS[5]]
                for half in range(2):
                    for fi in range(2):
                        for c in range(22):
                            kc = half * 22 + c
                            k.mm(Ys[fi][:, :], sls[half][:, c, fi * 128:(fi + 1) * 128], act_[kc].all, start=(kc == 0), stop=(kc == NF - 1))
                for fi in range(2):
                    f = fg2 * 2 + fi
                    y_evac(Ys[fi], ysb[f], sqrot, ssb, f)
            tail = gen_tail(ysb, ssb, rstd, xres, t0, TB, 3, xrot, gn_l[l])
        run_gen(tail)
        k.barrier()
        k.take_sems(slr.bufs + xrot.bufs + [rstd])
        st.close()

    for l in range(NL):
        prep_small(l)
    st0 = ExitStack()
    caster.engs = ("dve", "act")
    caster.alloc(st0)
    add_cast_jobs_early(0)
    caster.finish([("w_in", 0)])
    k.barrier()
    caster.release()
    st0.close()
    caster.engs = ("dve",)
    for l in range(NL):
        phase1(l, xin if l == 0 else xres)
        if stop_after == ("p1", l):
            break
        add_cast_jobs_rest(l)
        if l + 1 < NL:
            add_cast_jobs_early(l + 1)
        phase2(l)
        if stop_after == ("p2", l):
            break
        phase3(l, xin if l == 0 else xres)
        if stop_after == ("p3", l):
            break
        phase4(l)

    k.barrier()
    gs.close()
    k.es.close()
    return nc


def host_inputs(inputs, b, S=4096, NL=4):
    f = np.float32
    NT = S // 128
    m = {}
    m["xT"] = np.ascontiguousarray(np.asarray(inputs["x"])[b, :S].T).astype(f)
    for n in ["w_in", "w_gate", "w_out", "w_branch_fox", "w_branch_diff", "w_branch_nsa", "w_ffn_up", "w_ffn_down",
              "nsa_cmp_w1", "nsa_cmp_w2"]:
        m[n] = np.ascontiguousarray(np.asarray(inputs[n])[:NL]).astype(f)
    g = np.asarray(inputs["norm_gains"])[:NL]
    m["gains"] = np.ascontiguousarray(g.reshape(NL, 4, KC, 128).transpose(0, 1, 3, 2)).astype(f)
    m["fbias"] = np.ascontiguousarray(np.asarray(inputs["fox_forget_bias"])[:NL].reshape(NL, 4, 1)).astype(f)
    m["dlam"] = np.ascontiguousarray(np.asarray(inputs["diff_lambda"])[:NL].reshape(NL, 256)).astype(f)
    m["subln"] = np.ascontiguousarray(np.asarray(inputs["diff_subln"])[:NL].reshape(NL, 128, 1)).astype(f)
    m["cposT"] = np.ascontiguousarray(np.asarray(inputs["nsa_cmp_pos"])[:NL].transpose(0, 1, 3, 2)).astype(f)
    m["c_ident"] = np.eye(128, dtype=f)
    ncmp = (S - 32) // 16 + 1
    c = np.arange(256)
    j = np.arange(64)
    ov = ((c[:, None] * 16 < j[None, :] * 64 + 64) & (c[:, None] * 16 + 31 >= j[None, :] * 64) & (c[:, None] < ncmp)).astype(f)
    m["c_ov"] = np.ascontiguousarray(ov.reshape(2, 128, 64).transpose(1, 0, 2))
    kt = np.arange(NT)
    p = np.arange(128)
    e = ((128 * kt[None, :, None] + p[None, None, :]) // 64 == j[:, None, None]).astype(f)
    m["c_ekt"] = np.ascontiguousarray(e)
    lt = np.zeros((3, 48, 128), f)
    lt[0] = p[None, :]
    lt[1] = 1.0
    lt[2] = (np.arange(48) - 31)[:, None]
    m["c_lt"] = lt
    rt = np.zeros((3, 8, 512), f)
    jb = np.arange(512) // 128
    for h in range(4):
        sl = 2.0 ** (-2.0 * (h + 1))
        rt[0, h] = sl
        rt[1, h] = -sl * (128 * jb + 64)
        rt[2, h] = 128 * sl
    for g in range(2):
        slc = 2.0 ** (-(4 * g + jb + 1.0))
        rt[0, 4 + g] = slc
        rt[1, 4 + g] = -64 * slc
        rt[2, 4 + g] = 128 * slc
        rt[0, 6 + g] = 16 * slc
        rt[1, 6 + g] = -33 * slc
        rt[2, 6 + g] = 128 * slc
    m["c_rt"] = rt
    oh = np.zeros((4, 4, 128), f)
    for h in range(4):
        oh[h, h] = 1.0
    m["c_oneh"] = oh
    sidx = np.arange(S)
    m["c_ka"] = np.stack([sidx % 128, np.ones(S), sidx // 128]).astype(f)
    m["c_qa"] = np.stack([np.ones(S), -(128 * (sidx // 128) + 64), 128 * np.ones(S)]).astype(f)
    ab = np.zeros((3, 2, 2, 4), f)
    for g in range(2):
        slc = 2.0 ** (-(4 * g + np.arange(4) + 1.0))
        ab[0, g, 0] = slc
        ab[1, g, 0] = -64 * slc
        ab[2, g, 0] = 128 * slc
        ab[1, g, 1] = -128 * slc
    m["c_selab"] = ab
    return m


def kernel(**inputs):
    S, NL = 4096, 4
    B = inputs["x"].shape[0]
    nc = build(S, NL)
    in_maps = [host_inputs(inputs, b, S, NL) for b in range(B)]
    res = run_bass_kernel_spmd(nc, in_maps, core_ids=list(range(B)))
    out = np.stack([np.ascontiguousarray(r["yT"].T) for r in res.results], axis=0)
    return out.astype(np.float32)
```
